# Optimizing a Trainium2 kernel written in Bass

```python
import math
import jax, jax.numpy as jnp
from jax import lax
import numpy as np

D_MODEL = 4096
BATCH = 2
SEQ = 8192
DEPTH = 2

GRID_W = 64
CTX_LEN = 256
HEAD_DIM = 128
ROPE_THETA = 10000.0
Q_BLOCK = 128
EPS = 1e-6
N_MOD = 6

A_Q_HEADS = 16
A_KV_HEADS = 4
A_GROUP = A_Q_HEADS // A_KV_HEADS
A_Q_DIM = A_Q_HEADS * HEAD_DIM
A_KV_DIM = A_KV_HEADS * HEAD_DIM
B_WIDTH = 2048
CONV_W = 3
C_HEADS = 8
C_QK_DIM = C_HEADS * 2 * HEAD_DIM
C_V_DIM = C_HEADS * 2 * HEAD_DIM

N_BRANCH = 3
BRANCH_DIM = 2048
KV_COLS = 2 * A_KV_DIM + C_QK_DIM + C_V_DIM
REST_COLS = A_Q_DIM + C_QK_DIM + 3 * B_WIDTH + N_BRANCH * D_MODEL
IN_COLS = KV_COLS + REST_COLS
D_FF = 2 * D_MODEL

kernel_name = "hybrid_parallel_gqa_shortconv_diffattn_dit"


def rms_norm(x, g):
    xf = x.astype(jnp.float32)
    y = xf * lax.rsqrt(jnp.mean(xf * xf, axis=-1, keepdims=True) + EPS)
    return (y * g.astype(jnp.float32)).astype(x.dtype)


def modulate(h, shift, scale):
    return h * (1 + scale) + shift


def axial_rope_tables(rows):
    row = jnp.broadcast_to(jnp.arange(rows, dtype=jnp.float32)[:, None], (rows, GRID_W)).reshape(-1)
    col = jnp.broadcast_to(jnp.arange(GRID_W, dtype=jnp.float32)[None, :], (rows, GRID_W)).reshape(-1)
    quarter = HEAD_DIM // 4
    freqs = ROPE_THETA ** (-jnp.arange(quarter, dtype=jnp.float32) / quarter)
    ang_r = row[:, None] * freqs
    ang_c = col[:, None] * freqs
    ang = jnp.concatenate([ang_r, ang_r, ang_c, ang_c], axis=-1)
    return jnp.cos(ang), jnp.sin(ang)


def apply_rope(x, cos, sin):
    shape = (1, x.shape[1]) + (1,) * (x.ndim - 3) + (HEAD_DIM,)
    xf = x.astype(jnp.float32)
    xs = xf.reshape(xf.shape[:-1] + (2, 2, HEAD_DIM // 4))
    rot = jnp.stack([-xs[..., 1, :], xs[..., 0, :]], axis=-2).reshape(xf.shape)
    return (xf * cos.reshape(shape) + rot * sin.reshape(shape)).astype(x.dtype)


def dwconv3(x, w):
    return lax.conv_general_dilated(
        x, w[:, None, :].astype(x.dtype), window_strides=(1,),
        padding=((CONV_W // 2, CONV_W // 2),),
        dimension_numbers=('NWC', 'WIO', 'NWC'), feature_group_count=x.shape[-1])


def gqa_attend(q, k, v):
    s = jnp.einsum('bqhgd,bkhd->bhgqk', q, k) * (HEAD_DIM ** -0.5)
    p = jax.nn.softmax(s.astype(jnp.float32), axis=-1).astype(v.dtype)
    return jnp.einsum('bhgqk,bkhd->bqhgd', p, v)


def diff_attend(q, k, v, lam):
    s = jnp.einsum('bqhjd,bkhjd->bhjqk', q, k) * (HEAD_DIM ** -0.5)
    p = jax.nn.softmax(s.astype(jnp.float32), axis=-1)
    p = (p[:, :, 0] - lam * p[:, :, 1]).astype(v.dtype)
    return jnp.einsum('bhqk,bkhe->bqhe', p, v)


def blocked_queries(attend, q, *rest):
    b, s = q.shape[:2]
    nblk = s // Q_BLOCK
    qb = jnp.moveaxis(q.reshape((b, nblk, Q_BLOCK) + q.shape[2:]), 1, 0)
    ob = lax.map(lambda blk: attend(blk, *rest), qb)
    return jnp.moveaxis(ob, 0, 1).reshape((b, s) + ob.shape[3:])


def split_kv(p_kv, g_ka, g_kc, rope):
    b, t = p_kv.shape[:2]
    ak, av, ck, cv = jnp.split(p_kv, [A_KV_DIM, 2 * A_KV_DIM, 2 * A_KV_DIM + C_QK_DIM], axis=-1)
    ak = rms_norm(ak.reshape(b, t, A_KV_HEADS, HEAD_DIM), g_ka)
    ck = rms_norm(ck.reshape(b, t, C_HEADS, 2, HEAD_DIM), g_kc)
    if rope is not None:
        ak = apply_rope(ak, *rope)
        ck = apply_rope(ck, *rope)
    return (ak, av.reshape(b, t, A_KV_HEADS, HEAD_DIM), ck, cv.reshape(b, t, C_HEADS, 2 * HEAD_DIM))


def mixer_output(p_rest, kv, rope, g_qa, g_qc, w_conv_b, lam, lambda_init, g_subln, w_branch, w_out):
    b, t = p_rest.shape[:2]
    aq, cq, bb, bc, bx, gates = jnp.split(
        p_rest, [A_Q_DIM, A_Q_DIM + C_QK_DIM, A_Q_DIM + C_QK_DIM + B_WIDTH,
                 A_Q_DIM + C_QK_DIM + 2 * B_WIDTH, A_Q_DIM + C_QK_DIM + 3 * B_WIDTH], axis=-1)
    ak, av, ck, cv = kv
    aq = rms_norm(aq.reshape(b, t, A_KV_HEADS, A_GROUP, HEAD_DIM), g_qa)
    cq = rms_norm(cq.reshape(b, t, C_HEADS, 2, HEAD_DIM), g_qc)
    if rope is not None:
        aq = apply_rope(aq, *rope)
        cq = apply_rope(cq, *rope)
        ya = blocked_queries(gqa_attend, aq, ak, av)
        yc = blocked_queries(lambda qb, kk, vv: diff_attend(qb, kk, vv, lam), cq, ck, cv)
    else:
        ya = gqa_attend(aq, ak, av)
        yc = diff_attend(cq, ck, cv, lam)
    ya = ya.reshape(b, t, A_Q_DIM)
    yc = (rms_norm(yc, g_subln) * (1 - lambda_init)).reshape(b, t, C_V_DIM)
    yb = bb * dwconv3(bc * bx, w_conv_b)
    branches = jnp.stack([ya, yb, yc], axis=2)
    proj = jnp.einsum('btnc,ncd->btnd', branches, w_branch)
    g = jax.nn.sigmoid(gates.reshape(b, t, N_BRANCH, D_MODEL))
    return jnp.sum(g * proj, axis=2) @ w_out


def conv_glu(h, w_up, w_conv_ffn, w_down):
    u, g = jnp.split(h @ w_up, 2, axis=-1)
    return (jax.nn.silu(dwconv3(g, w_conv_ffn)) * u) @ w_down


def setup_inputs(seed: int = 0) -> dict:
    key = jax.random.key(seed)
    ks = jax.random.split(key, 24)
    L, D = DEPTH, D_MODEL

    def nrm(k, shape, scale):
        return jax.random.normal(k, shape, jnp.float32) * scale

    return {
        "x": nrm(ks[0], (BATCH, SEQ, D), 1.0),
        "c": nrm(ks[1], (BATCH, D), 1.0),
        "ctx": nrm(ks[2], (BATCH, CTX_LEN, D), 1.0),
        "c_ctx": nrm(ks[3], (D,), 1.0),
        "w_mod": nrm(ks[4], (L, D, N_MOD * D), 0.5 * D ** -0.5),
        "b_mod": nrm(ks[5], (L, N_MOD * D), 0.02),
        "g_norm_mix": 1.0 + nrm(ks[6], (L, D), 0.02),
        "g_norm_ffn": 1.0 + nrm(ks[7], (L, D), 0.02),
        "w_in": nrm(ks[8], (L, D, IN_COLS), D ** -0.5),
        "g_qa": 1.0 + nrm(ks[9], (L, HEAD_DIM), 0.02),
        "g_ka": 1.0 + nrm(ks[10], (L, HEAD_DIM), 0.02),
        "g_qc": 1.0 + nrm(ks[11], (L, 2, HEAD_DIM), 0.02),
        "g_kc": 1.0 + nrm(ks[12], (L, 2, HEAD_DIM), 0.02),
        "w_conv_b": nrm(ks[13], (L, CONV_W, B_WIDTH), CONV_W ** -0.5),
        "lam_qk": nrm(ks[14], (L, 4, HEAD_DIM), 0.1),
        "g_subln": 1.0 + nrm(ks[15], (L, 2 * HEAD_DIM), 0.02),
        "w_branch": nrm(ks[16], (L, N_BRANCH, BRANCH_DIM, D), BRANCH_DIM ** -0.5),
        "w_out": nrm(ks[17], (L, D, D), D ** -0.5),
        "w_up": nrm(ks[18], (L, D, 2 * D_FF), D ** -0.5),
        "w_conv_ffn": nrm(ks[19], (L, CONV_W, D_FF), CONV_W ** -0.5),
        "w_down": nrm(ks[20], (L, D_FF, D), D_FF ** -0.5),
    }


def reference(x, c, ctx, c_ctx, w_mod, b_mod, g_norm_mix, g_norm_ffn, w_in, g_qa, g_ka, g_qc, g_kc,
              w_conv_b, lam_qk, g_subln, w_branch, w_out, w_up, w_conv_ffn, w_down):
    n_tok = x.shape[1]
    rows = n_tok // GRID_W
    rope = axial_rope_tables(rows)
    s_c = jax.nn.silu(c)
    s_ctx = jax.nn.silu(c_ctx)
    for l in range(DEPTH):
        last = l == DEPTH - 1
        lambda_init = 0.8 - 0.6 * math.exp(-0.3 * l)
        lq = lam_qk[l].astype(jnp.float32)
        lam = jnp.exp(jnp.sum(lq[0] * lq[1])) - jnp.exp(jnp.sum(lq[2] * lq[3])) + lambda_init

        mx = (s_c @ w_mod[l] + b_mod[l])[:, None, :]
        sh_m, sc_m, gt_m, sh_f, sc_f, gt_f = jnp.split(mx, N_MOD, axis=-1)
        mc = s_ctx @ w_mod[l] + b_mod[l]
        csh_m, csc_m, cgt_m, csh_f, csc_f, cgt_f = jnp.split(mc, N_MOD, axis=-1)

        h_x = modulate(rms_norm(x, g_norm_mix[l]), sh_m, sc_m)
        h_c = modulate(rms_norm(ctx, g_norm_mix[l]), csh_m, csc_m)
        p_x = h_x @ w_in[l]
        if last:
            p_c_kv = h_c @ w_in[l][:, :KV_COLS]
        else:
            p_c = h_c @ w_in[l]
            p_c_kv = p_c[..., :KV_COLS]
        kv_c = split_kv(p_c_kv, g_ka[l], g_kc[l], None)
        kv_x = split_kv(p_x[..., :KV_COLS], g_ka[l], g_kc[l], rope)
        kv_all = tuple(jnp.concatenate([kc_, kx_], axis=1) for kc_, kx_ in zip(kv_c, kv_x))
        mix_x = mixer_output(p_x[..., KV_COLS:], kv_all, rope, g_qa[l], g_qc[l], w_conv_b[l],
                             lam, lambda_init, g_subln[l], w_branch[l], w_out[l])
        x = x + gt_m * mix_x
        if not last:
            mix_c = mixer_output(p_c[..., KV_COLS:], kv_c, None, g_qa[l], g_qc[l], w_conv_b[l],
                                 lam, lambda_init, g_subln[l], w_branch[l], w_out[l])
            ctx = ctx + cgt_m * mix_c

        f_x = modulate(rms_norm(x, g_norm_ffn[l]), sh_f, sc_f)
        x = x + gt_f * conv_glu(f_x, w_up[l], w_conv_ffn[l], w_down[l])
        if not last:
            f_c = modulate(rms_norm(ctx, g_norm_ffn[l]), csh_f, csc_f)
            ctx = ctx + cgt_f * conv_glu(f_c, w_up[l], w_conv_ffn[l], w_down[l])
    return x
```

```python
import math
import os
from contextlib import ExitStack
import numpy as np
import concourse.bass as bass
import concourse.mybir as mybir
from concourse.bass_utils import run_bass_kernel_spmd

F32 = mybir.dt.float32
BF16 = mybir.dt.bfloat16
AF = mybir.ActivationFunctionType
ALU = mybir.AluOpType
P = 128
EPS = 1e-6
GW = 64
ROPE_THETA = 10000.0
NCORE = 8
DEPTH = 2


class Cfg:
    def __init__(s, D=4096, S=8192, CTX=256, AQH=16, AKVH=4, BW=2048, CH=8, DFF=8192):
        s.D, s.S, s.CTX, s.AQH, s.AKVH, s.BW, s.CH, s.DFF = D, S, CTX, AQH, AKVH, BW, CH, DFF
        s.TL = S // 4
        s.AQ = AQH * P
        s.AKV = AKVH * P
        s.CQK = CH * 2 * P
        s.CV = CH * 2 * P
        s.BR = s.AQ
        assert s.BW == s.BR and s.CV == s.BR
        s.KVC = 2 * s.AKV + s.CQK + s.CV
        s.REST = s.AQ + s.CQK + 3 * BW + 3 * D
        s.INC = s.KVC + s.REST
        s.G = AQH // AKVH
        s.KC = D // P
        s.NKT = AKVH + 2 * CH
        s.NQT = AQH + 2 * CH
        s.VC = s.AKV + s.CV
        s.NB = 512
        assert s.TL % 512 == 0 and CTX % P == 0 and CTX <= 512
        assert (D // 8) % P == 0 and (DFF // 8) % P == 0 and (3 * s.BR // 8) % P == 0


class Slot:
    __slots__ = ("w", "r")

    def __init__(s):
        s.w = {}
        s.r = {}


def _merge(dst, src):
    for k, (sem, v) in src.items():
        if k not in dst or dst[k][1] < v:
            dst[k] = (sem, v)


class Tracker:
    ND = 8

    def __init__(self, nc, es):
        self.nc = nc
        self.es = es
        self.names = ["pe", "act", "dve", "pool", "sp"]
        self.ops = {k: [] for k in self.names}
        self.msem = {k: es.enter_context(nc.semaphore("m_" + k)) for k in self.names}
        self.mcnt = {k: 0 for k in self.names}
        self.mold = []
        self.MAXC = int(os.environ.get("MK_MAXC", "60000"))
        self.dsem = {q: [es.enter_context(nc.semaphore("d_%s%d" % (q, i))) for i in range(self.ND)]
                     for q in ("sp", "pool", "act")}
        self.dcnt = {q: [0] * self.ND for q in ("sp", "pool", "act")}
        self.dnext = {"sp": 0, "pool": 0, "act": 0}
        self.NCC = 16
        self.ccs = [es.enter_context(nc.semaphore("cc%d" % i)) for i in range(self.NCC)]
        self.cccnt = [0] * self.NCC
        self.ccnext = 0
        self.seen = {k: {} for k in self.names}
        self.slots = {}

    def slot(self, key):
        s = self.slots.get(key)
        if s is None:
            s = self.slots[key] = Slot()
        return s

    def _emit(self, eng, fn, waits, inc):
        seen = self.seen[eng]
        wl = []
        for k, (sem, v) in waits.items():
            if seen.get(k, 0) < v:
                seen[k] = v
                wl.append((sem, v))
        self.ops[eng].append((wl, fn, inc))

    def _deps(self, reads, writes):
        waits = {}
        for s in reads:
            _merge(waits, s.w)
        for s in writes:
            _merge(waits, s.w)
            _merge(waits, s.r)
        return waits

    def _reg(self, tok, reads, writes):
        k = id(tok[0])
        for s in reads:
            s.r[k] = tok
        for s in writes:
            s.w = {k: tok}
            s.r = {}

    def _bump(self, eng):
        if self.mcnt[eng] >= self.MAXC:
            self.mold.append((self.msem[eng], self.mcnt[eng]))
            self.msem[eng] = self.es.enter_context(self.nc.semaphore("m_%s_%d" % (eng, len(self.mold))))
            self.mcnt[eng] = 0
        self.mcnt[eng] += 1

    def op(self, eng, meth, kw, reads=(), writes=()):
        fn = (meth, kw)
        waits = self._deps(reads, writes)
        self._bump(eng)
        tok = (self.msem[eng], self.mcnt[eng])
        self._emit(eng, fn, waits, (self.msem[eng], 1))
        self._reg(tok, reads, writes)
        return tok

    def mm_group(self, ps_slot, items, first=True, final=True, meth="matmul"):
        n = len(items)
        allr = []
        tok = None
        for i, (kw, rs) in enumerate(items):
            waits = {}
            for s in rs:
                _merge(waits, s.w)
            if i == 0 and first:
                _merge(waits, ps_slot.w)
                _merge(waits, ps_slot.r)
            allr += list(rs)
            if i < n - 1:
                self._emit("pe", (meth, kw), waits, None)
            else:
                self._bump("pe")
                tok = (self.msem["pe"], self.mcnt["pe"])
                self._emit("pe", (meth, kw), waits, (self.msem["pe"], 1))
        self._reg(tok, allr, [ps_slot] if final else [])
        return tok

    def dma(self, q, out, in_, reads=(), writes=()):
        i = self.dnext[q]
        self.dnext[q] = (i + 1) % self.ND
        sem = self.dsem[q][i]
        prev = self.dcnt[q][i]
        waits = self._deps(reads, writes)
        if prev:
            _merge(waits, {id(sem): (sem, prev)})
        self.dcnt[q][i] = prev + 16
        tok = (sem, prev + 16)
        self._emit(q, ("dma_start", dict(out=out, in_=in_, allow_slow_non_contiguous=True)), waits, (sem, 16))
        self._reg(tok, reads, writes)
        return tok

    def cc(self, groups, in_ap, out_ap, reads=(), writes=()):
        i = self.ccnext
        self.ccnext = (i + 1) % self.NCC
        sem = self.ccs[i]
        prev = self.cccnt[i]
        waits = self._deps(reads, writes)
        if prev:
            _merge(waits, {id(sem): (sem, prev)})
        self.cccnt[i] = prev + 1
        tok = (sem, prev + 1)
        self._emit("pool", ("_cc", dict(groups=groups, in_ap=in_ap, out_ap=out_ap)), waits, (sem, 1))
        self._reg(tok, reads, writes)
        return tok

    def barrier(self, full=False):
        toks = {}
        for k in self.names:
            if self.mcnt[k]:
                toks[id(self.msem[k])] = (self.msem[k], self.mcnt[k])
        for sem, v in self.mold:
            toks[id(sem)] = (sem, v)
        for q in (("sp", "pool", "act") if full else ("sp", "act")):
            for i in range(self.ND):
                if self.dcnt[q][i]:
                    toks[id(self.dsem[q][i])] = (self.dsem[q][i], self.dcnt[q][i])
        if full:
            for i, sem in enumerate(self.ccs):
                if self.cccnt[i]:
                    toks[id(sem)] = (sem, self.cccnt[i])
        for k in self.names:
            self._emit(k, None, dict(toks), None)

    def replay(self):
        nc = self.nc
        engs = {"pe": "tensor", "act": "scalar", "dve": "vector", "pool": "gpsimd", "sp": "sync"}
        with nc.Block() as block:
            for k in self.names:
                ops = self.ops[k]

                def run(e, ops=ops):
                    for wl, fn, inc in ops:
                        for sem, v in wl:
                            e.wait_ge(sem, v)
                        if fn is not None:
                            meth, kw = fn
                            if meth == "_cc":
                                ins = e.collective_compute("AllGather", ALU.bypass, replica_groups=kw["groups"],
                                                           ins=[kw["in_ap"]], outs=[kw["out_ap"]])
                            else:
                                ins = getattr(e, meth)(**kw)
                            if inc is not None:
                                ins.then_inc(inc[0], inc[1])
                getattr(block, engs[k])(run)


_UID = [0]


class Ring:
    def __init__(self, T, es, nc, name, shape, dt, n, psum=False):
        alloc = nc.psum_tensor if psum else nc.sbuf_tensor
        _UID[0] += 1
        self.bufs = [es.enter_context(alloc("r_%s_%d_%d" % (name, _UID[0], i), list(shape), dt)) for i in range(n)]
        self.slots = [Slot() for _ in range(n)]
        self.i = 0
        self.n = n

    def next(self):
        b, s = self.bufs[self.i], self.slots[self.i]
        self.i = (self.i + 1) % self.n
        return b, s


def _largest_div(n, cap, mult=1):
    best = None
    for d in range(1, n + 1):
        if n % d == 0 and d <= cap and d % mult == 0:
            best = d
    assert best is not None, (n, cap, mult)
    return best


def wchunks(R, C):
    r1 = _largest_div(R, (1 << 20) // (2 * C))
    m2 = _largest_div(4 * R, (2 << 20) // (2 * C))
    return r1, m2


def shard_rows(R, C, grp, rank):
    r1, m2 = wchunks(R, C)
    s_ = np.arange(R)
    i, t = s_ // r1, s_ % r1
    a = i * 4 * r1 + rank * r1 + t
    return (a // m2) * 2 * m2 + grp * m2 + (a % m2)


def _pow2div(c, cap=2048):
    b = cap
    while c % b:
        b //= 2
    return b


def build(cfg):
    nc = bass.Bass("TRN2", target_bir_lowering=False)
    D, TL, CTX, KC = cfg.D, cfg.TL, cfg.CTX, cfg.KC
    BW, BR, DFF, INC = cfg.BW, cfg.BR, cfg.DFF, cfg.INC
    AKV, CQK, CV, AQ, KVC, VC = cfg.AKV, cfg.CQK, cfg.CV, cfg.AQ, cfg.KVC, cfg.VC
    NKT, NQT, CH, AQH, AKVH, G = cfg.NKT, cfg.NQT, cfg.CH, cfg.AQH, cfg.AKVH, cfg.G
    L = DEPTH
    TT = CTX + TL + 2
    TA = CTX + TL
    D8, F8, B8 = D // 8, DFF // 8, 3 * BR // 8
    kc8 = D8 // P
    NKEY = CTX + 4 * TL
    NKEYT = NKEY // P
    SCALE = float(P) ** -0.5
    TPC = _largest_div(NKT, (1 << 20) // (2 * TL * P))
    assert VC * P * 2 <= (1 << 20)

    def din(name, shape, dt=F32):
        return nc.dram_tensor(name, list(shape), dt, kind="ExternalInput").ap()

    def dsc(name, shape, dt=BF16):
        return nc.dram_tensor(name, list(shape), dt, kind="Internal").ap()

    x_in = din("x", [TL, D])
    xh_in = din("xh", [2, D])
    ctx_in = din("ctx", [CTX, D])
    cT_in = din("cT", [P, kc8, 3])
    wmod_in = din("w_mod", [L, D8, 6 * D])
    bmod_in = din("b_mod", [L, 1, 6 * D])
    selB_in = din("selB", [25, 2, P])
    selH_in = din("selH", [8, 2])
    hmask_in = din("hmask", [P, 2])
    gmix_in = din("g_mix", [L, P, D])
    gffn_in = din("g_ffn", [L, P, D])
    gq_in = din("gq", [P, L * 3])
    gk_in = din("gk", [P, L * 3])
    wcb_in = din("wcb", [P, L, 3, BW // P])
    wcf_in = din("wcf", [P, L, 3, DFF // P])
    lamq_in = din("lamq", [P, L * 4])
    gsub_in = din("gsub", [P, L * 2])
    cos_in = din("cosT", [P, TL])
    sin_in = din("sinT", [P, TL])
    cst_in = din("consts", [3, P, P])
    win_in = din("w_in", [L, D8, INC])
    wbr_in = din("w_br", [L, B8, D])
    wout_in = din("w_out", [L, D8, D])
    wup_in = din("w_up", [L, D8, 2 * DFF])
    wdn_in = din("w_down", [L, F8, D])
    y_out = nc.dram_tensor("y", [TL, D], F32, kind="ExternalOutput").ap()

    xm = dsc("xm", [TA, D], F32)
    xo = dsc("xo", [TA, D], F32)
    xhs = dsc("xhs", [2, D], F32)
    hT = dsc("hT", [D, TT])
    KT_loc = dsc("KT_loc", [NKT * P, TL])
    KT_all = dsc("KT_all", [4 * NKT * P, TL])
    KT_ctx = dsc("KT_ctx", [NKT * P, CTX])
    V_loc = dsc("V_loc", [TL, VC])
    V_all = dsc("V_all", [4 * TL, VC])
    V_ctx = dsc("V_ctx", [CTX, VC])
    QT = dsc("QT", [NQT * P, TA])
    bbT = dsc("bbT", [BW, TA])
    zT = dsc("zT", [BW, TL + 2], F32)
    zTc = dsc("zTc", [BW, CTX + 2], F32)
    sgT = dsc("sgT", [3 * D, TA])
    yT = dsc("yT", [3 * BR, TA])
    mT = dsc("mT", [D, TA])
    uT = dsc("uT", [DFF, TA])
    gT = dsc("gT", [DFF, TL + 2], F32)
    gTc = dsc("gTc", [DFF, CTX + 2], F32)
    aT = dsc("aT", [DFF, TA])
    modpart = dsc("modpart", [3, 6 * D], F32)
    modmid = dsc("modmid", [12, 6 * D], F32)
    modall = dsc("modall", [24, 6 * D], F32)
    modbc = dsc("modbc", [L * 12, P, D], F32)
    hb_in = dsc("hb_in", [2, D], F32)
    hb_all = dsc("hb_all", [8, D], F32)
    wspec = [("w_in", win_in, D8, INC), ("w_br", wbr_in, B8, D), ("w_out", wout_in, D8, D),
             ("w_up", wup_in, D8, 2 * DFF), ("w_down", wdn_in, F8, D)]
    Wb = {}
    for l in range(L):
        for nm, _, R, C in wspec:
            Wb[(nm, l)] = (dsc("%s_s%d" % (nm, l), [R, C]), dsc("%s_m%d" % (nm, l), [4 * R, C]),
                           dsc("%s_f%d" % (nm, l), [8 * R, C]))

    G4 = [[0, 1, 2, 3], [4, 5, 6, 7]]
    G2 = [[0, 4], [1, 5], [2, 6], [3, 7]]

    es = ExitStack()
    with es:
        T = Tracker(nc, es)
        sl = T.slot

        def sb(st, name, shape, dt=F32):
            _UID[0] += 1
            return st.enter_context(nc.sbuf_tensor("sb_%s_%d" % (name, _UID[0]), list(shape), dt))

        def ps(st, name, shape=(P, 512), dt=F32):
            _UID[0] += 1
            return st.enter_context(nc.psum_tensor("ps_%s_%d" % (name, _UID[0]), list(shape), dt))

        cbf = sb(es, "cbf", [P, 3, P], BF16)
        ones32 = sb(es, "ones32", [P, P], F32)
        gq = sb(es, "gq", [P, L * 3])
        gk = sb(es, "gk", [P, L * 3])
        wcb = sb(es, "wcb", [P, L, 3, BW // P])
        wcf = sb(es, "wcf", [P, L, 3, DFF // P])
        lamq = sb(es, "lamq", [P, L * 4])
        gsub = sb(es, "gsub", [P, L * 2])
        hmask = sb(es, "hmask", [P, 2])
        selH = sb(es, "selH", [8, 2])
        selB = sb(es, "selB", [25, 2, P])
        zeros = sb(es, "zeros", [P, 64])
        nlam = sb(es, "nlam", [P, L])
        gsc = sb(es, "gsc", [P, L * 2])
        S_c = sl("consts")
        T.dma("pool", cbf[:, :, :], cst_in.rearrange("c p m -> p c m"), writes=[S_c])
        T.dma("sp", ones32[:, :], cst_in[2], writes=[S_c])
        for t_, a_ in ((gq, gq_in), (gk, gk_in), (lamq, lamq_in), (gsub, gsub_in), (hmask, hmask_in),
                       (selH, selH_in)):
            T.dma("sp", t_[:, :], a_, writes=[S_c])
        T.dma("sp", wcb[:, :, :, :], wcb_in, writes=[S_c])
        T.dma("sp", wcf[:, :, :, :], wcf_in, writes=[S_c])
        T.dma("sp", selB[:, :, :], selB_in, writes=[S_c])
        T.op("dve", "memset", dict(ap=zeros[:, :], constant=0.0), writes=[S_c])
        T.dma("sp", zTc.rearrange("(c p) t -> p c t", p=P)[:, :, 0:1], zeros[:, 0:BW // P].unsqueeze(2),
              reads=[S_c], writes=[sl("zTc")])
        T.dma("sp", zTc.rearrange("(c p) t -> p c t", p=P)[:, :, CTX + 1:CTX + 2],
              zeros[:, 0:BW // P].unsqueeze(2), reads=[S_c], writes=[sl("zTc")])
        T.dma("sp", gTc.rearrange("(c p) t -> p c t", p=P)[:, :, 0:1], zeros[:, 0:DFF // P].unsqueeze(2),
              reads=[S_c], writes=[sl("gTc")])
        T.dma("sp", gTc.rearrange("(c p) t -> p c t", p=P)[:, :, CTX + 1:CTX + 2],
              zeros[:, 0:DFF // P].unsqueeze(2), reads=[S_c], writes=[sl("gTc")])

        with ExitStack() as st:
            pr = sb(st, "lam_pr", [P, 2 * L])
            ex = sb(st, "lam_ex", [P, 2 * L])
            pl = ps(st, "lam_ps", [P, 2 * L])
            s_pr, s_ex, s_pl = Slot(), Slot(), Slot()
            lq4 = lamq[:, :].rearrange("p (l f) -> p l f", f=4)
            pr3 = pr[:, :].rearrange("p (l f) -> p l f", f=2)
            T.op("dve", "tensor_tensor", dict(out=pr3[:, :, 0], in0=lq4[:, :, 0], in1=lq4[:, :, 1], op=ALU.mult),
                 reads=[S_c], writes=[s_pr])
            T.op("dve", "tensor_tensor", dict(out=pr3[:, :, 1], in0=lq4[:, :, 2], in1=lq4[:, :, 3], op=ALU.mult),
                 reads=[S_c], writes=[s_pr])
            T.mm_group(s_pl, [(dict(out=pl[:, :], lhsT=ones32[:, :], rhs=pr[:, :], start=True, stop=True),
                               [s_pr, S_c])])
            T.op("act", "activation", dict(out=ex[:, :], in_=pl[:, :], func=AF.Exp), reads=[s_pl], writes=[s_ex])
            for l in range(L):
                li = 0.8 - 0.6 * math.exp(-0.3 * l)
                T.op("dve", "tensor_tensor", dict(out=nlam[:, l:l + 1], in0=ex[:, 2 * l + 1:2 * l + 2],
                                                           in1=ex[:, 2 * l:2 * l + 1], op=ALU.subtract),
                     reads=[s_ex], writes=[S_c])
                T.op("dve", "tensor_scalar", dict(out=nlam[:, l:l + 1], in0=nlam[:, l:l + 1],
                                                                  scalar1=-li, scalar2=None, op0=ALU.add),
                     reads=[S_c], writes=[S_c])
                T.op("dve", "tensor_scalar", dict(out=gsc[:, 2 * l:2 * l + 2], in0=gsub[:, 2 * l:2 * l + 2],
                                                                  scalar1=1.0 - li, scalar2=None, op0=ALU.mult),
                     reads=[S_c], writes=[S_c])
            T.barrier()

        Wslots = {}

        def weights_phase(l):
            for nm, src, R, C in wspec:
                sh, mid, full = Wb[(nm, l)]
                b = _pow2div(C)
                r1, m2 = wchunks(R, C)
                S_sh = sl((nm, l, "s"))
                T.dma("pool", sh.rearrange("r (a b) -> (r a) b", b=b), src[l].rearrange("r (a b) -> (r a) b", b=b),
                      writes=[S_sh])
                n1 = R // r1
                for i in range(n1):
                    T.cc(G4, sh[i * r1:(i + 1) * r1, :], mid[i * 4 * r1:(i + 1) * 4 * r1, :], reads=[S_sh],
                         writes=[sl((nm, l, "m", i))])
                n2 = 4 * R // m2
                fs = []
                for j in range(n2):
                    i0, i1 = (j * m2) // (4 * r1), ((j + 1) * m2 - 1) // (4 * r1)
                    fsl = sl((nm, l, "f", j))
                    T.cc(G2, mid[j * m2:(j + 1) * m2, :], full[j * 2 * m2:(j + 1) * 2 * m2, :],
                         reads=[sl((nm, l, "m", i)) for i in range(i0, i1 + 1)], writes=[fsl])
                    fs.append(fsl)
                Wslots[(nm, l)] = fs


        lat_blocks = [dict(kind="lat", t0=i * 512, n=512, col=CTX + i * 512) for i in range(TL // 512)]
        ctx_block = dict(kind="ctx", t0=0, n=CTX, col=0)
        halo_block = dict(kind="halo", t0=0, n=2, col=CTX + TL)

        def mod_phase(l):
            with ExitStack() as st:
                sT = sb(st, "sT", [P, kc8, 3])
                wm = Ring(T, st, nc, "wm", [P, kc8, 512], F32, 2)
                pp = Ring(T, st, nc, "modps", [P, 512], F32, 2, psum=True)
                stg = Ring(T, st, nc, "modst", [P, 512], F32, 2)
                s_sT = Slot()
                T.dma("sp", sT[:, :, :], cT_in, writes=[s_sT])
                T.op("act", "activation", dict(out=sT[:, :, :], in_=sT[:, :, :], func=AF.Silu),
                     reads=[], writes=[s_sT])
                S_mp = sl("modpart")
                for cg in range(6 * D // 512):
                    wt, ws = wm.next()
                    T.dma("sp", wt[:, :, :], wmod_in[l].rearrange("(k p) c -> p k c", p=P)[:, :, cg * 512:(cg + 1) * 512],
                          writes=[ws])
                    pt, pss = pp.next()
                    T.mm_group(pss, [(dict(out=pt[0:3, :], lhsT=sT[:, k, :], rhs=wt[:, k, :],
                                                                             start=(k == 0), stop=(k == kc8 - 1)),
                                      [s_sT, ws]) for k in range(kc8)])
                    so, sos = stg.next()
                    T.op("act", "activation", dict(out=so[0:3, :], in_=pt[0:3, :], func=AF.Copy),
                         reads=[pss], writes=[sos])
                    T.dma("act", modpart[:, cg * 512:(cg + 1) * 512], so[0:3, :], reads=[sos], writes=[S_mp])
                T.cc(G4, modpart, modmid, reads=[S_mp], writes=[sl("modmid")])
                T.cc(G2, modmid, modall, reads=[sl("modmid")], writes=[sl("modall")])
                gm = sb(st, "gmix", [P, D])
                gf = sb(st, "gffn", [P, D])
                s_g = Slot()
                T.dma("sp", gm[:, :], gmix_in[l], writes=[s_g])
                T.dma("sp", gf[:, :], gffn_in[l], writes=[s_g])
                ge = Ring(T, st, nc, "ge", [25, 512], F32, 2)
                for sec in range(6):
                    for ch in range(D // 512):
                        c0 = sec * D + ch * 512
                        gt, gs = ge.next()
                        T.dma("sp", gt[0:24, :], modall[:, c0:c0 + 512], reads=[sl("modall")], writes=[gs])
                        T.dma("sp", gt[24:25, :], bmod_in[l][:, c0:c0 + 512], writes=[gs])
                        for v in range(2):
                            pt, pss = pp.next()
                            T.mm_group(pss, [(dict(out=pt[:, :], lhsT=selB[:, v, :], rhs=gt[:, :],
                                                                                     start=True, stop=True), [gs, S_c])])
                            so, sos = stg.next()
                            if sec in (1, 4):
                                gg = gm if sec == 1 else gf
                                T.op("dve", "scalar_tensor_tensor", dict(
                                    out=so[:, :], in0=pt[:, :], scalar=1.0, in1=gg[:, ch * 512:(ch + 1) * 512],
                                    op0=ALU.add, op1=ALU.mult), reads=[pss, s_g], writes=[sos])
                            else:
                                T.op("act", "activation", dict(out=so[:, :], in_=pt[:, :], func=AF.Copy),
                                     reads=[pss], writes=[sos])
                            T.dma("act", modbc[l * 12 + v * 6 + sec][:, ch * 512:(ch + 1) * 512], so[:, :], reads=[sos],
                                  writes=[sl(("modbc", l * 12 + v * 6 + sec))])
                T.barrier()

        def norm_phase(l, which, src_of, blocks):
            secA, secB = (1, 0) if which == "mix" else (4, 3)
            with ExitStack() as st:
                A = sb(st, "nA", [P, D])
                Bt = sb(st, "nB", [P, D])
                s_ab = Slot()
                xr = Ring(T, st, nc, "nx", [P, D], F32, 2)
                tmp = sb(st, "ntmp", [P, D])
                hb = sb(st, "nhb", [P, D], BF16)
                junk = sb(st, "njunk", [P, D], BF16)
                st4 = Ring(T, st, nc, "nst", [P, 8], F32, 2)
                hTs = Ring(T, st, nc, "nhTs", [P, KC, 512], BF16, 1)
                pTr = Ring(T, st, nc, "npT", [P, KC, P], BF16, 2, psum=True)
                s_tmp, s_hb, s_junk = Slot(), Slot(), Slot()
                cur_v = None
                for blk in blocks:
                    v = 1 if blk["kind"] == "ctx" else 0
                    if v != cur_v:
                        T.dma("sp", A[:, :], modbc[l * 12 + v * 6 + secA], reads=[sl(("modbc", l * 12 + v * 6 + secA))], writes=[s_ab])
                        T.dma("sp", Bt[:, :], modbc[l * 12 + v * 6 + secB], reads=[sl(("modbc", l * 12 + v * 6 + secB))], writes=[s_ab])
                        cur_v = v
                    src = src_of(blk)
                    n = blk["n"]
                    ht, hts = hTs.next()
                    for sub in range((n + P - 1) // P):
                        r = min(P, n - sub * P)
                        xt, xs = xr.next()
                        T.dma("sp", xt[0:r, :], src[0][sub * P:sub * P + r, :], reads=src[1], writes=[xs])
                        s4, s4s = st4.next()
                        T.op("act", "activation", dict(out=junk[0:r, :], in_=xt[0:r, :], func=AF.Square,
                                                                              accum_out=s4[0:r, 0:1]),
                             reads=[xs], writes=[s_junk, s4s])
                        T.op("dve", "tensor_scalar", dict(out=s4[0:r, 1:2], in0=s4[0:r, 0:1], scalar1=1.0 / D,
                                                                          scalar2=EPS, op0=ALU.mult, op1=ALU.add),
                             reads=[], writes=[s4s])
                        T.op("act", "activation", dict(out=s4[0:r, 2:3], in_=s4[0:r, 1:2], func=AF.Sqrt),
                             reads=[], writes=[s4s])
                        T.op("dve", "reciprocal", dict(out=s4[0:r, 3:4], in_=s4[0:r, 2:3]),
                             reads=[], writes=[s4s])
                        T.op("dve", "scalar_tensor_tensor", dict(
                            out=tmp[0:r, :], in0=xt[0:r, :], scalar=s4[0:r, 3:4], in1=A[0:r, :], op0=ALU.mult, op1=ALU.mult),
                            reads=[xs, s4s, s_ab], writes=[s_tmp])
                        T.op("dve", "tensor_tensor", dict(out=hb[0:r, :], in0=tmp[0:r, :], in1=Bt[0:r, :], op=ALU.add),
                             reads=[s_tmp, s_ab], writes=[s_hb])
                        pT, pTs = pTr.next()
                        T.mm_group(pTs, [(dict(out=pT[:, k, 0:r], in_=hb[0:r, k * P:(k + 1) * P],
                                                                                 identity=cbf[0:r, 0, 0:r]), [s_hb, S_c])
                                         for k in range(KC)], meth="transpose")
                        T.op("act", "activation", dict(
                            out=ht[:, :, sub * P:sub * P + r], in_=pT[:, :, 0:r], func=AF.Copy), reads=[pTs], writes=[hts])
                    T.dma("act", hT.rearrange("(k p) t -> p k t", p=P)[:, :, blk["col"]:blk["col"] + n], ht[:, :, 0:n],
                          reads=[hts], writes=[sl(("hT", blk["col"]))])
                T.barrier()

        def proj_fm(Wfull, S_w, Krows, actT, act_slot_of, blocks, groups, epilogue, k0=0):
            kc = Krows // P
            with ExitStack() as st:
                wr = Ring(T, st, nc, "pfw", [P, kc, 512], BF16, 2)
                ar = Ring(T, st, nc, "pfa", [P, kc, 512], BF16, 2)
                pr_ = Ring(T, st, nc, "pfps", [P, 512], F32, 4, psum=True)
                ctxo = epilogue("init", st)
                for blk in blocks:
                    n = blk["n"]
                    at, as_ = ar.next()
                    T.dma("sp", at[:, :, 0:n], actT.rearrange("(k p) t -> p k t", p=P)[:, k0:k0 + kc, blk["col"]:blk["col"] + n],
                          reads=[act_slot_of(blk)], writes=[as_])
                    for c0, tags in groups(blk):
                        wt, ws = wr.next()
                        w = len(tags) * P
                        T.dma("sp", wt[:, :, 0:w], Wfull.rearrange("(k p) c -> p k c", p=P)[:, k0:k0 + kc, c0:c0 + w],
                              reads=list(S_w), writes=[ws])
                        for ti, tag in enumerate(tags):
                            pt, pss = pr_.next()
                            T.mm_group(pss, [(dict(out=pt[:, 0:n], lhsT=wt[:, k, ti * P:(ti + 1) * P], rhs=at[:, k, 0:n],
                                start=(k == 0), stop=(k == kc - 1)), [ws, as_]) for k in range(kc)])
                            epilogue(ctxo, blk, tag, pt, pss)
                T.barrier()

        def proj_tm(Wfull, S_w, Krows, actT, act_slot_of, blocks, colgroups, epilogue):
            kc = Krows // P
            nslab = (kc + 31) // 32
            with ExitStack() as st:
                wr = Ring(T, st, nc, "ptw", [P, 32, 512], BF16, 2)
                at = sb(st, "pta", [P, kc, 512], BF16)
                as_ = Slot()
                pr_ = Ring(T, st, nc, "ptps", [P, 512], F32, 8, psum=True)
                ctxo = epilogue("init", st)
                for blk in blocks:
                    n = blk["n"]
                    nsub = (n + P - 1) // P
                    T.dma("sp", at[:, :, 0:n], actT.rearrange("(k p) t -> p k t", p=P)[:, :, blk["col"]:blk["col"] + n],
                          reads=[act_slot_of(blk)], writes=[as_])
                    for wc0, oc0, w in colgroups:
                        accs = [pr_.next() for _ in range(nsub)]
                        pend = [[] for _ in range(nsub)]
                        for s_i in range(nslab):
                            ks = min(32, kc - s_i * 32)
                            wt, ws = wr.next()
                            T.dma("sp", wt[:, 0:ks, 0:w],
                                  Wfull.rearrange("(k p) c -> p k c", p=P)[:, s_i * 32:s_i * 32 + ks, wc0:wc0 + w],
                                  reads=list(S_w), writes=[ws])
                            for sub in range(nsub):
                                r = min(P, n - sub * P)
                                pt = accs[sub][0]
                                for k in range(ks):
                                    kk = s_i * 32 + k
                                    pend[sub].append((dict(out=pt[0:r, 0:w], lhsT=at[:, kk, sub * P:sub * P + r], rhs=wt[:, k, 0:w],
                                        start=(kk == 0), stop=(kk == kc - 1)), [ws, as_]))
                            last_slab = (s_i == nslab - 1)
                            for sub in range(nsub):
                                items = pend[sub]
                                pend[sub] = []
                                T.mm_group(accs[sub][1], items, first=(s_i == 0), final=last_slab)
                                if last_slab:
                                    r = min(P, n - sub * P)
                                    epilogue(ctxo, blk, sub, r, (wc0, oc0, w), accs[sub][0], accs[sub][1])
                T.barrier()

        def hT_slot(blk):
            return sl(("hT", blk["col"]))

        def wslice(l, nm):
            return Wb[(nm, l)][2], Wslots[(nm, l)]

        def qk_phase(l, is_q, blocks):
            Wf, S_w = wslice(l, "w_in")
            if is_q:
                base = KVC
                tiles = [(base + i * P, ("q", i, 0)) for i in range(AQH)] + \
                        [(base + AQ + i * P, ("q", AQH + i, 1 + (i % 2))) for i in range(2 * CH)]
                gtile = gq
            else:
                tiles = [(i * P, ("k", i, 0)) for i in range(AKVH)] + \
                        [(2 * AKV + i * P, ("k", AKVH + i, 1 + (i % 2))) for i in range(2 * CH)]
                gtile = gk

            def groups(blk):
                out = []
                i = 0
                while i < len(tiles):
                    j = i
                    while j + 1 < len(tiles) and j + 1 - i < 4 and tiles[j + 1][0] == tiles[j][0] + P:
                        j += 1
                    out.append((tiles[i][0], [t[1] for t in tiles[i:j + 1]]))
                    i = j + 1
                return out

            def epi(c, blk=None, tag=None, pt=None, pss=None):
                if c == "init":
                    st = blk
                    o = dict(
                        sq=Ring(T, st, nc, "qsq", [P, 512], BF16, 2), ssp=Ring(T, st, nc, "qssp", [P, 512], F32, 2, psum=True),
                        ln=Ring(T, st, nc, "qln", [P, 512], F32, 2), rr=Ring(T, st, nc, "qrr", [P, 512], F32, 2),
                        kn=Ring(T, st, nc, "qkn", [P, 512], BF16, 2), rot=Ring(T, st, nc, "qrot", [P, 512], F32, 2, psum=True),
                        t1=Ring(T, st, nc, "qt1", [P, 512], F32, 2), t2=Ring(T, st, nc, "qt2", [P, 512], F32, 2),
                        ko=Ring(T, st, nc, "qko", [P, 512], BF16, 3),
                        cos=sb(st, "qcos", [P, 512]), sin=sb(st, "qsin", [P, 512]), s_cs=Slot(), cs_t0=[None])
                    return o
                n = blk["n"]
                _, ti, gi = tag
                rope = blk["kind"] == "lat"
                sq, sqs = c["sq"].next()
                T.op("act", "activation", dict(out=sq[:, 0:n], in_=pt[:, 0:n], func=AF.Square), reads=[pss], writes=[sqs])
                sp_, sps = c["ssp"].next()
                T.mm_group(sps, [(dict(out=sp_[:, 0:n], lhsT=cbf[:, 2, :], rhs=sq[:, 0:n], start=True, stop=True),
                                  [sqs, S_c])])
                ln, lns = c["ln"].next()
                T.op("act", "activation", dict(out=ln[:, 0:n], in_=sp_[:, 0:n], func=AF.Ln, scale=1.0 / P, bias=epsb[:, 0:1]),
                     reads=[sps, S_c], writes=[lns])
                rr, rrs = c["rr"].next()
                T.op("act", "activation", dict(out=rr[:, 0:n], in_=ln[:, 0:n], func=AF.Exp, scale=-0.5), reads=[lns], writes=[rrs])
                col = l * 3 + gi
                if not rope:
                    ko, kos = c["ko"].next()
                    T.op("dve", "scalar_tensor_tensor", dict(out=ko[:, 0:n], in0=pt[:, 0:n], scalar=gtile[:, col:col + 1],
                                                                 in1=rr[:, 0:n], op0=ALU.mult, op1=ALU.mult),
                         reads=[pss, rrs, S_c], writes=[kos])
                else:
                    if c["cs_t0"][0] != blk["t0"]:
                        T.dma("sp", c["cos"][:, 0:n], cos_in[:, blk["t0"]:blk["t0"] + n], writes=[c["s_cs"]])
                        T.dma("sp", c["sin"][:, 0:n], sin_in[:, blk["t0"]:blk["t0"] + n], writes=[c["s_cs"]])
                        c["cs_t0"][0] = blk["t0"]
                    kn, kns = c["kn"].next()
                    T.op("dve", "scalar_tensor_tensor", dict(out=kn[:, 0:n], in0=pt[:, 0:n], scalar=gtile[:, col:col + 1],
                                                                 in1=rr[:, 0:n], op0=ALU.mult, op1=ALU.mult),
                         reads=[pss, rrs, S_c], writes=[kns])
                    rp, rps = c["rot"].next()
                    T.mm_group(rps, [(dict(out=rp[:, 0:n], lhsT=cbf[:, 1, :], rhs=kn[:, 0:n], start=True, stop=True),
                                      [kns, S_c])])
                    t1, t1s = c["t1"].next()
                    T.op("dve", "tensor_tensor", dict(out=t1[:, 0:n], in0=kn[:, 0:n], in1=c["cos"][:, 0:n], op=ALU.mult),
                         reads=[kns, c["s_cs"]], writes=[t1s])
                    t2, t2s = c["t2"].next()
                    T.op("dve", "tensor_tensor", dict(out=t2[:, 0:n], in0=rp[:, 0:n], in1=c["sin"][:, 0:n], op=ALU.mult),
                         reads=[rps, c["s_cs"]], writes=[t2s])
                    ko, kos = c["ko"].next()
                    T.op("dve", "tensor_tensor", dict(out=ko[:, 0:n], in0=t1[:, 0:n], in1=t2[:, 0:n], op=ALU.add),
                         reads=[t1s, t2s], writes=[kos])
                if is_q:
                    dst, dsl = QT[ti * P:(ti + 1) * P, blk["col"]:blk["col"] + n], sl(("QT", blk["col"]))
                elif blk["kind"] == "ctx":
                    dst, dsl = KT_ctx[ti * P:(ti + 1) * P, 0:n], sl("KT_ctx")
                else:
                    dst, dsl = KT_loc[ti * P:(ti + 1) * P, blk["t0"]:blk["t0"] + n], sl("KT_loc")
                T.dma("act", dst, ko[:, 0:n], reads=[kos], writes=[dsl])

            proj_fm(Wf, S_w, D, hT, hT_slot, blocks, groups, epi)

        def v_phase(l, blocks):
            Wf, S_w = wslice(l, "w_in")
            cgs = []
            for (w0, o0, tot) in ((AKV, 0, AKV), (2 * AKV + CQK, AKV, CV)):
                c = 0
                while c < tot:
                    w = min(512, tot - c)
                    cgs.append((w0 + c, o0 + c, w))
                    c += w

            def epi(c, blk=None, sub=None, r=None, cg=None, pt=None, pss=None):
                if c == "init":
                    return dict(vo=Ring(T, blk, nc, "vvo", [P, 512], BF16, 3))
                wc0, oc0, w = cg
                vo, vos = c["vo"].next()
                T.op("act", "activation", dict(out=vo[0:r, 0:w], in_=pt[0:r, 0:w], func=AF.Copy), reads=[pss], writes=[vos])
                if blk["kind"] == "ctx":
                    dst, dsl = V_ctx[sub * P:sub * P + r, oc0:oc0 + w], sl("V_ctx")
                else:
                    dst, dsl = V_loc[blk["t0"] + sub * P:blk["t0"] + sub * P + r, oc0:oc0 + w], sl("V_loc")
                T.dma("act", dst, vo[0:r, 0:w], reads=[vos], writes=[dsl])

            proj_tm(Wf, S_w, D, hT, hT_slot, blocks, cgs, epi)

        def restb_phase(l, blocks):
            Wf, S_w = wslice(l, "w_in")
            base = KVC + AQ + CQK
            nct = BW // P

            def groups(blk):
                out = []
                for c in range(0, nct, 4):
                    m = min(4, nct - c)
                    if blk["kind"] != "halo":
                        out.append((base + c * P, [("bb", c + i) for i in range(m)]))
                    out.append((base + BW + c * P, [("bc", c + i) for i in range(m)]))
                    out.append((base + 2 * BW + c * P, [("bx", c + i) for i in range(m)]))
                return out

            def epi(c, blk=None, tag=None, pt=None, pss=None):
                if c == "init":
                    st = blk
                    return dict(bbo=Ring(T, st, nc, "rbb", [P, 512], BF16, 3), bc=Ring(T, st, nc, "rbc", [P, 512], F32, 5),
                                z=Ring(T, st, nc, "rz", [P, 512], F32, 3), held={})
                n = blk["n"]
                kind, ci = tag
                if kind == "bb":
                    o, os_ = c["bbo"].next()
                    T.op("act", "activation", dict(out=o[:, 0:n], in_=pt[:, 0:n], func=AF.Copy), reads=[pss], writes=[os_])
                    T.dma("act", bbT[ci * P:(ci + 1) * P, blk["col"]:blk["col"] + n], o[:, 0:n], reads=[os_],
                          writes=[sl(("bbT", blk["col"]))])
                elif kind == "bc":
                    o, os_ = c["bc"].next()
                    T.op("act", "activation", dict(out=o[:, 0:n], in_=pt[:, 0:n], func=AF.Copy), reads=[pss], writes=[os_])
                    c["held"][ci] = (o, os_)
                else:
                    bc_, bcs = c["held"].pop(ci)
                    z, zs = c["z"].next()
                    T.op("dve", "tensor_tensor", dict(out=z[:, 0:n], in0=pt[:, 0:n], in1=bc_[:, 0:n], op=ALU.mult),
                         reads=[pss, bcs], writes=[zs])
                    rows = slice(ci * P, (ci + 1) * P)
                    if blk["kind"] == "halo":
                        T.op("dve", "tensor_tensor", dict(out=z[:, 0:2], in0=z[:, 0:2], in1=hmask[:, 0:2], op=ALU.mult),
                             reads=[S_c], writes=[zs])
                        T.dma("act", zT[rows, 0:1], z[:, 0:1], reads=[zs], writes=[sl("zT")])
                        T.dma("act", zT[rows, TL + 1:TL + 2], z[:, 1:2], reads=[zs], writes=[sl("zT")])
                    elif blk["kind"] == "ctx":
                        T.dma("act", zTc[rows, 1:1 + n], z[:, 0:n], reads=[zs], writes=[sl("zTc")])
                    else:
                        T.dma("act", zT[rows, 1 + blk["t0"]:1 + blk["t0"] + n], z[:, 0:n], reads=[zs], writes=[sl("zT")])

            proj_fm(Wf, S_w, D, hT, hT_slot, blocks, groups, epi)

        def gates_phase(l, blocks):
            Wf, S_w = wslice(l, "w_in")
            base = KVC + AQ + CQK + 3 * BW
            nt = 3 * D // P

            def groups(blk):
                return [(base + c * P, [c + i for i in range(min(4, nt - c))]) for c in range(0, nt, 4)]

            def epi(c, blk=None, tag=None, pt=None, pss=None):
                if c == "init":
                    return dict(o=Ring(T, blk, nc, "gso", [P, 512], BF16, 3))
                n = blk["n"]
                o, os_ = c["o"].next()
                T.op("act", "activation", dict(out=o[:, 0:n], in_=pt[:, 0:n], func=AF.Sigmoid), reads=[pss], writes=[os_])
                T.dma("act", sgT[tag * P:(tag + 1) * P, blk["col"]:blk["col"] + n], o[:, 0:n], reads=[os_],
                      writes=[sl(("sgT", blk["col"]))])

            proj_fm(Wf, S_w, D, hT, hT_slot, blocks, groups, epi)

        def conv_phase(l, blocks, which):
            if which == "b":
                nct, wct, src, srcc, mul, mulname, dst, drow0 = BW // P, wcb, zT, zTc, bbT, "bbT", yT, BR
            else:
                nct, wct, src, srcc, mul, mulname, dst, drow0 = DFF // P, wcf, gT, gTc, uT, "uT", aT, 0
            with ExitStack() as st:
                win = Ring(T, st, nc, "cwin", [P, 516], F32, 3)
                mu = Ring(T, st, nc, "cmu", [P, 512], BF16, 3)
                a1 = Ring(T, st, nc, "ca1", [P, 512], F32, 2)
                a2 = Ring(T, st, nc, "ca2", [P, 512], F32, 2)
                a3 = Ring(T, st, nc, "ca3", [P, 512], F32, 2)
                a4 = Ring(T, st, nc, "ca4", [P, 512], F32, 2)
                oo = Ring(T, st, nc, "coo", [P, 512], BF16, 3)
                for blk in blocks:
                    n = blk["n"]
                    isctx = blk["kind"] == "ctx"
                    for ci in range(nct):
                        rows = slice(ci * P, (ci + 1) * P)
                        wt, ws = win.next()
                        if isctx:
                            T.dma("sp", wt[:, 0:n + 2], srcc[rows, 0:n + 2], reads=[sl("zTc" if which == "b" else "gTc")], writes=[ws])
                        else:
                            T.dma("sp", wt[:, 0:n + 2], src[rows, blk["t0"]:blk["t0"] + n + 2],
                                  reads=[sl("zT" if which == "b" else "gT")], writes=[ws])
                        mt, ms = mu.next()
                        T.dma("sp", mt[:, 0:n], mul[rows, blk["col"]:blk["col"] + n], reads=[sl((mulname, blk["col"]))], writes=[ms])
                        b1, b1s = a1.next()
                        T.op("dve", "tensor_scalar", dict(out=b1[:, 0:n], in0=wt[:, 0:n], scalar1=wct[:, l, 0, ci:ci + 1],
                                                                                  scalar2=None, op0=ALU.mult), reads=[ws, S_c], writes=[b1s])
                        b2, b2s = a2.next()
                        T.op("dve", "scalar_tensor_tensor", dict(
                            out=b2[:, 0:n], in0=wt[:, 1:n + 1], scalar=wct[:, l, 1, ci:ci + 1], in1=b1[:, 0:n], op0=ALU.mult, op1=ALU.add),
                            reads=[ws, b1s, S_c], writes=[b2s])
                        b3, b3s = a3.next()
                        T.op("dve", "scalar_tensor_tensor", dict(
                            out=b3[:, 0:n], in0=wt[:, 2:n + 2], scalar=wct[:, l, 2, ci:ci + 1], in1=b2[:, 0:n], op0=ALU.mult, op1=ALU.add),
                            reads=[ws, b2s, S_c], writes=[b3s])
                        o, os_ = oo.next()
                        if which == "f":
                            b4, b4s = a4.next()
                            T.op("act", "activation", dict(out=b4[:, 0:n], in_=b3[:, 0:n], func=AF.Silu),
                                 reads=[b3s], writes=[b4s])
                            b3, b3s = b4, b4s
                        T.op("dve", "tensor_tensor", dict(out=o[:, 0:n], in0=b3[:, 0:n], in1=mt[:, 0:n], op=ALU.mult),
                             reads=[b3s, ms], writes=[os_])
                        T.dma("act", dst[drow0 + ci * P:drow0 + (ci + 1) * P, blk["col"]:blk["col"] + n], o[:, 0:n], reads=[os_],
                              writes=[sl(("yT" if which == "b" else "aT", blk["col"]))])
                T.barrier()

        def attn_phase(l, blocks):
            with ExitStack() as st:
                Kr = Ring(T, st, nc, "aK", [P, 2, NKEY], BF16, 2)
                Vr = Ring(T, st, nc, "aV", [P, NKEYT, 256], BF16, 1)
                Qr = Ring(T, st, nc, "aQ", [P, max(G, 2), 512], BF16, 2)
                Sp = Ring(T, st, nc, "aS", [P, 512], F32, 4, psum=True)
                Pt = Ring(T, st, nc, "aP", [P, 512], BF16, 6)
                Oa = Ring(T, st, nc, "aOa", [P, 512], F32, 1, psum=True)
                Ob = Ring(T, st, nc, "aOb", [P, 512], F32, 1, psum=True)
                Sm = Ring(T, st, nc, "aSm", [P, 512], F32, 1, psum=True)
                Lp = Ring(T, st, nc, "aLp", [P, 512], F32, 1, psum=True)
                rs = Ring(T, st, nc, "ars", [P, 512], F32, 2)
                f1 = Ring(T, st, nc, "af1", [P, 2, 512], F32, 2)
                f2 = Ring(T, st, nc, "af2", [P, 2, 512], F32, 2)
                tq = Ring(T, st, nc, "atq", [P, 512], F32, 2)
                sqr = Ring(T, st, nc, "asq", [P, 2, 512], BF16, 2)
                lnr = Ring(T, st, nc, "aln", [P, 512], F32, 2)
                yo = Ring(T, st, nc, "ayo", [P, 512], BF16, 3)
                KA = KT_all.rearrange("(i r j d) t -> d i r j t", r=4, j=TPC, d=P)
                VA = V_all.rearrange("(i r p) c -> p r i c", r=4, p=P)
                VCx = V_ctx.rearrange("(k p) c -> p k c", p=P)
                nkc = CTX // P
                NI = TL // P
                S_ka = [sl(("KT_all", ci)) for ci in range(NKT // TPC)]
                S_va = [sl(("V_all", ci)) for ci in range(TL // P)]

                def load_K(Kt, Ks, j, tile):
                    T.dma("sp", Kt[:, j, 0:CTX], KT_ctx[tile * P:(tile + 1) * P, :], reads=[sl("KT_ctx")], writes=[Ks])
                    for r in range(4):
                        T.dma("sp", Kt[:, j, CTX + r * TL:CTX + (r + 1) * TL], KA[:, tile // TPC, r, tile % TPC, :],
                              reads=[S_ka[tile // TPC]], writes=[Ks])

                def load_V(Vt, Vs, c0, w):
                    T.dma("sp", Vt[:, 0:nkc, 0:w], VCx[:, :, c0:c0 + w], reads=[sl("V_ctx")], writes=[Vs])
                    for r in range(4):
                        T.dma("sp", Vt[:, nkc + r * NI:nkc + (r + 1) * NI, 0:w], VA[:, r, :, c0:c0 + w], reads=S_va, writes=[Vs])

                def softmax_pv(Kt, Ks, j, Vt, Vs, Qt, Qs, qi, n, nkt, dvs):
                    accs = [Oa.next(), Ob.next()][:len(dvs)]
                    sm, sms = Sm.next()
                    for kt in range(nkt):
                        sp_, sps = Sp.next()
                        T.op("pe", "matmul", dict(out=sp_[:, 0:n], lhsT=Kt[:, j, kt * P:(kt + 1) * P], rhs=Qt[:, qi, 0:n],
                                                                      start=True, stop=True), reads=[Ks, Qs], writes=[sps])
                        pt, pts = Pt.next()
                        T.op("act", "activation", dict(out=pt[:, 0:n], in_=sp_[:, 0:n], func=AF.Exp, scale=SCALE),
                             reads=[sps], writes=[pts])
                        fl = (kt == 0 or kt == nkt - 1)
                        for (acc, accs_), dv in zip(accs, dvs):
                            T.op("pe", "matmul", dict(out=acc[:, 0:n], lhsT=Vt[:, kt, dv:dv + P], rhs=pt[:, 0:n],
                                                                                       start=(kt == 0), stop=(kt == nkt - 1)),
                                 reads=[pts, Vs], writes=[accs_] if fl else [])
                        T.op("pe", "matmul", dict(out=sm[:, 0:n], lhsT=cbf[:, 2, :], rhs=pt[:, 0:n],
                                                                          start=(kt == 0), stop=(kt == nkt - 1)),
                             reads=[pts, S_c], writes=[sms] if fl else [])
                    return accs, (sm, sms)

                for hk in range(AKVH):
                    Kt, Ks = Kr.next()
                    load_K(Kt, Ks, 0, hk)
                    Vt, Vs = Vr.next()
                    load_V(Vt, Vs, hk * P, P)
                    for blk in blocks:
                        n = blk["n"]
                        nkt = nkc if blk["kind"] == "ctx" else NKEYT
                        Qt, Qs = Qr.next()
                        T.dma("sp", Qt[:, 0:G, 0:n],
                              QT.rearrange("(j d) t -> d j t", d=P)[:, hk * G:(hk + 1) * G, blk["col"]:blk["col"] + n],
                              reads=[sl(("QT", blk["col"]))], writes=[Qs])
                        for g in range(G):
                            accs, (sm, sms) = softmax_pv(Kt, Ks, 0, Vt, Vs, Qt, Qs, g, n, nkt, [0])
                            r_, rs_ = rs.next()
                            T.op("dve", "reciprocal", dict(out=r_[:, 0:n], in_=sm[:, 0:n]), reads=[sms], writes=[rs_])
                            o, os_ = yo.next()
                            acc, accs_ = accs[0]
                            T.op("dve", "tensor_tensor", dict(out=o[:, 0:n], in0=acc[:, 0:n], in1=r_[:, 0:n], op=ALU.mult),
                                 reads=[accs_, rs_], writes=[os_])
                            qi = hk * G + g
                            T.dma("act", yT[qi * P:(qi + 1) * P, blk["col"]:blk["col"] + n], o[:, 0:n], reads=[os_],
                                  writes=[sl(("yT", blk["col"]))])
                for h in range(CH):
                    Kt, Ks = Kr.next()
                    load_K(Kt, Ks, 0, AKVH + 2 * h)
                    load_K(Kt, Ks, 1, AKVH + 2 * h + 1)
                    Vt, Vs = Vr.next()
                    load_V(Vt, Vs, AKV + h * 256, 256)
                    for blk in blocks:
                        n = blk["n"]
                        nkt = nkc if blk["kind"] == "ctx" else NKEYT
                        Qt, Qs = Qr.next()
                        T.dma("sp", Qt[:, 0:2, 0:n],
                              QT.rearrange("(j d) t -> d j t", d=P)[:, AQH + 2 * h:AQH + 2 * h + 2, blk["col"]:blk["col"] + n],
                              reads=[sl(("QT", blk["col"]))], writes=[Qs])
                        accs, (sm, sms) = softmax_pv(Kt, Ks, 0, Vt, Vs, Qt, Qs, 0, n, nkt, [0, P])
                        r_, rs_ = rs.next()
                        T.op("dve", "reciprocal", dict(out=r_[:, 0:n], in_=sm[:, 0:n]), reads=[sms], writes=[rs_])
                        o1, o1s = f1.next()
                        for hf in range(2):
                            acc, accs_ = accs[hf]
                            T.op("dve", "tensor_tensor", dict(out=o1[:, hf, 0:n], in0=acc[:, 0:n], in1=r_[:, 0:n], op=ALU.mult),
                                 reads=[accs_, rs_], writes=[o1s])
                        accs, (sm, sms) = softmax_pv(Kt, Ks, 1, Vt, Vs, Qt, Qs, 1, n, nkt, [0, P])
                        r2, r2s = rs.next()
                        T.op("dve", "reciprocal", dict(out=r2[:, 0:n], in_=sm[:, 0:n]), reads=[sms], writes=[r2s])
                        o2, o2s = f2.next()
                        sq, sqs = sqr.next()
                        for hf in range(2):
                            acc, accs_ = accs[hf]
                            t_, ts_ = tq.next()
                            T.op("dve", "tensor_tensor", dict(out=t_[:, 0:n], in0=acc[:, 0:n], in1=r2[:, 0:n], op=ALU.mult),
                                 reads=[accs_, r2s], writes=[ts_])
                            T.op("dve", "scalar_tensor_tensor", dict(
                                out=o2[:, hf, 0:n], in0=t_[:, 0:n], scalar=nlam[:, l:l + 1], in1=o1[:, hf, 0:n], op0=ALU.mult, op1=ALU.add),
                                reads=[ts_, o1s, S_c], writes=[o2s])
                            T.op("act", "activation", dict(out=sq[:, hf, 0:n], in_=o2[:, hf, 0:n], func=AF.Square),
                                 reads=[o2s], writes=[sqs])
                        lp, lps = Lp.next()
                        T.mm_group(lps, [(dict(out=lp[:, 0:n], lhsT=cbf[:, 2, :], rhs=sq[:, hf, 0:n],
                                                                                 start=(hf == 0), stop=(hf == 1)), [sqs, S_c]) for hf in range(2)])
                        ln, lns = lnr.next()
                        T.op("act", "activation", dict(out=ln[:, 0:n], in_=lp[:, 0:n], func=AF.Ln, scale=1.0 / 256, bias=epsb[:, 0:1]),
                             reads=[lps, S_c], writes=[lns])
                        rr, rrs = rs.next()
                        T.op("act", "activation", dict(out=rr[:, 0:n], in_=ln[:, 0:n], func=AF.Exp, scale=-0.5),
                             reads=[lns], writes=[rrs])
                        for hf in range(2):
                            o, os_ = yo.next()
                            T.op("dve", "scalar_tensor_tensor", dict(
                                out=o[:, 0:n], in0=o2[:, hf, 0:n], scalar=gsc[:, 2 * l + hf:2 * l + hf + 1], in1=rr[:, 0:n],
                                op0=ALU.mult, op1=ALU.mult), reads=[o2s, rrs, S_c], writes=[os_])
                            r0 = 2 * BR + h * 256 + hf * P
                            T.dma("act", yT[r0:r0 + P, blk["col"]:blk["col"] + n], o[:, 0:n], reads=[os_],
                                  writes=[sl(("yT", blk["col"]))])
                T.barrier()

        def merge_phase(l, blocks):
            Wf, S_w = wslice(l, "w_br")
            kb = BR // P
            with ExitStack() as st:
                wr = Ring(T, st, nc, "mw", [P, 3 * kb, 256], BF16, 2)
                at = sb(st, "ma", [P, 3 * kb, 512], BF16)
                as_ = Slot()
                pr_ = Ring(T, st, nc, "mps", [P, 512], F32, 6, psum=True)
                sg = Ring(T, st, nc, "msg", [P, 3, 512], BF16, 3)
                t0r = Ring(T, st, nc, "mt0", [P, 512], F32, 2)
                t1r = Ring(T, st, nc, "mt1", [P, 512], F32, 2)
                t2r = Ring(T, st, nc, "mt2", [P, 512], F32, 2)
                t3r = Ring(T, st, nc, "mt3", [P, 512], F32, 2)
                mo = Ring(T, st, nc, "mmo", [P, 512], BF16, 3)
                for blk in blocks:
                    n = blk["n"]
                    T.dma("sp", at[:, :, 0:n], yT.rearrange("(k p) t -> p k t", p=P)[:, :, blk["col"]:blk["col"] + n],
                          reads=[sl(("yT", blk["col"]))], writes=[as_])
                    for dt0 in range(0, D // P, 2):
                        wt, ws = wr.next()
                        T.dma("sp", wt[:, :, :], Wf.rearrange("(k p) c -> p k c", p=P)[:, :, dt0 * P:(dt0 + 2) * P], reads=list(S_w), writes=[ws])
                        for di in range(2):
                            dt_ = dt0 + di
                            pss_ = []
                            for nb in range(3):
                                pt, pss = pr_.next()
                                T.mm_group(pss, [(dict(out=pt[:, 0:n], lhsT=wt[:, nb * kb + k, di * P:(di + 1) * P], rhs=at[:, nb * kb + k, 0:n],
                                    start=(k == 0), stop=(k == kb - 1)), [ws, as_]) for k in range(kb)])
                                pss_.append((pt, pss))
                            sgt, sgs = sg.next()
                            T.dma("sp", sgt[:, :, 0:n], sgT.rearrange("(b j p) t -> p b j t", b=3, p=P)[:, :, dt_, blk["col"]:blk["col"] + n],
                                  reads=[sl(("sgT", blk["col"]))], writes=[sgs])
                            a0, a0s = t0r.next()
                            T.op("dve", "tensor_tensor", dict(out=a0[:, 0:n], in0=pss_[0][0][:, 0:n], in1=sgt[:, 0, 0:n], op=ALU.mult),
                                 reads=[pss_[0][1], sgs], writes=[a0s])
                            a1, a1s = t1r.next()
                            T.op("dve", "tensor_tensor", dict(out=a1[:, 0:n], in0=pss_[1][0][:, 0:n], in1=sgt[:, 1, 0:n], op=ALU.mult),
                                 reads=[pss_[1][1], sgs], writes=[a1s])
                            a2, a2s = t2r.next()
                            T.op("dve", "tensor_tensor", dict(out=a2[:, 0:n], in0=pss_[2][0][:, 0:n], in1=sgt[:, 2, 0:n], op=ALU.mult),
                                 reads=[pss_[2][1], sgs], writes=[a2s])
                            a3, a3s = t3r.next()
                            T.op("dve", "tensor_tensor", dict(out=a3[:, 0:n], in0=a0[:, 0:n], in1=a1[:, 0:n], op=ALU.add),
                                 reads=[a0s, a1s], writes=[a3s])
                            o, os_ = mo.next()
                            T.op("dve", "tensor_tensor", dict(out=o[:, 0:n], in0=a3[:, 0:n], in1=a2[:, 0:n], op=ALU.add),
                                 reads=[a3s, a2s], writes=[os_])
                            T.dma("act", mT[dt_ * P:(dt_ + 1) * P, blk["col"]:blk["col"] + n], o[:, 0:n], reads=[os_],
                                  writes=[sl(("mT", blk["col"]))])
                T.barrier()

        def resid_phase(l, nm, Krows, actT, actname, blocks, gsec, xold_of, xnew_of):
            Wf, S_w = wslice(l, nm)
            cgs = [(c, c, 512) for c in range(0, D, 512)]

            def epi(c, blk=None, sub=None, r=None, cg=None, pt=None, pss=None):
                if c == "init":
                    st = blk
                    o = dict(gr=Ring(T, st, nc, "rg", [P, 512], F32, 3), gcur=[None, None],
                             xo_=Ring(T, st, nc, "rxo", [P, 512], F32, 3), t=Ring(T, st, nc, "rt", [P, 512], F32, 2),
                             o=Ring(T, st, nc, "ro", [P, 512], F32, 3))
                    return o
                wc0, oc0, w = cg
                if sub == 0:
                    v = 1 if blk["kind"] == "ctx" else 0
                    gt, gs_ = c["gr"].next()
                    T.dma("sp", gt[:, :], modbc[l * 12 + v * 6 + gsec][:, oc0:oc0 + w], reads=[sl(("modbc", l * 12 + v * 6 + gsec))], writes=[gs_])
                    c["gcur"] = [gt, gs_]
                gt, gs_ = c["gcur"]
                xsrc, xsl = xold_of(blk)
                xdst, xdl = xnew_of(blk)
                xt, xs = c["xo_"].next()
                T.dma("sp", xt[0:r, :], xsrc[sub * P:sub * P + r, oc0:oc0 + w], reads=xsl, writes=[xs])
                t_, ts_ = c["t"].next()
                T.op("dve", "tensor_tensor", dict(out=t_[0:r, :], in0=pt[0:r, :], in1=gt[0:r, :], op=ALU.mult),
                     reads=[pss, gs_], writes=[ts_])
                o, os_ = c["o"].next()
                T.op("dve", "tensor_tensor", dict(out=o[0:r, :], in0=t_[0:r, :], in1=xt[0:r, :], op=ALU.add),
                     reads=[ts_, xs], writes=[os_])
                T.dma("act", xdst[sub * P:sub * P + r, oc0:oc0 + w], o[0:r, :], reads=[os_], writes=xdl)

            proj_tm(Wf, S_w, Krows, actT, lambda blk: sl((actname, blk["col"])), blocks, cgs, epi)

        def halo_phase(xsrc, xsl):
            with ExitStack() as st:
                ha = sb(st, "ha", [8, D])
                ho = sb(st, "ho", [2, D])
                hp = Ring(T, st, nc, "hps", [P, 512], F32, 2, psum=True)
                s_ha, s_ho = Slot(), Slot()
                T.dma("act", hb_in[0:1, :], xsrc[CTX:CTX + 1, :], reads=xsl, writes=[sl("hb_in")])
                T.dma("act", hb_in[1:2, :], xsrc[CTX + TL - 1:CTX + TL, :], reads=xsl, writes=[sl("hb_in")])
                T.cc(G4, hb_in, hb_all, reads=[sl("hb_in")], writes=[sl("hb_all")])
                T.dma("sp", ha[:, :], hb_all, reads=[sl("hb_all")], writes=[s_ha])
                for ch in range(D // 512):
                    pt, pss = hp.next()
                    T.mm_group(pss, [(dict(out=pt[0:2, :], lhsT=selH[:, :], rhs=ha[:, ch * 512:(ch + 1) * 512],
                                                                       start=True, stop=True), [s_ha, S_c])])
                    T.op("act", "activation", dict(out=ho[0:2, ch * 512:(ch + 1) * 512], in_=pt[0:2, :], func=AF.Copy),
                         reads=[pss], writes=[s_ho])
                T.dma("act", xhs, ho[:, :], reads=[s_ho], writes=[sl("xhs")])
                T.barrier()

        def up_phase(l, blocks):
            Wf, S_w = wslice(l, "w_up")
            nct = DFF // P

            def groups(blk):
                out = []
                for c in range(0, nct, 4):
                    m = min(4, nct - c)
                    if blk["kind"] != "halo":
                        out.append((c * P, [("u", c + i) for i in range(m)]))
                    out.append((DFF + c * P, [("g", c + i) for i in range(m)]))
                return out

            def epi(c, blk=None, tag=None, pt=None, pss=None):
                if c == "init":
                    st = blk
                    return dict(uo=Ring(T, st, nc, "uuo", [P, 512], BF16, 3), go=Ring(T, st, nc, "ugo", [P, 512], F32, 3))
                n = blk["n"]
                kind, ci = tag
                rows = slice(ci * P, (ci + 1) * P)
                if kind == "u":
                    o, os_ = c["uo"].next()
                    T.op("act", "activation", dict(out=o[:, 0:n], in_=pt[:, 0:n], func=AF.Copy), reads=[pss], writes=[os_])
                    T.dma("act", uT[rows, blk["col"]:blk["col"] + n], o[:, 0:n], reads=[os_], writes=[sl(("uT", blk["col"]))])
                else:
                    o, os_ = c["go"].next()
                    if blk["kind"] == "halo":
                        T.op("dve", "tensor_tensor", dict(out=o[:, 0:2], in0=pt[:, 0:2], in1=hmask[:, 0:2], op=ALU.mult),
                             reads=[pss, S_c], writes=[os_])
                        T.dma("act", gT[rows, 0:1], o[:, 0:1], reads=[os_], writes=[sl("gT")])
                        T.dma("act", gT[rows, TL + 1:TL + 2], o[:, 1:2], reads=[os_], writes=[sl("gT")])
                    else:
                        T.op("act", "activation", dict(out=o[:, 0:n], in_=pt[:, 0:n], func=AF.Copy), reads=[pss], writes=[os_])
                        if blk["kind"] == "ctx":
                            T.dma("act", gTc[rows, 1:1 + n], o[:, 0:n], reads=[os_], writes=[sl("gTc")])
                        else:
                            T.dma("act", gT[rows, 1 + blk["t0"]:1 + blk["t0"] + n], o[:, 0:n], reads=[os_], writes=[sl("gT")])

            proj_fm(Wf, S_w, D, hT, hT_slot, blocks, groups, epi)

        epsb = sb(es, "epsb", [P, 1])
        T.op("dve", "memset", dict(ap=epsb[:, :], constant=EPS), writes=[S_c])

        def rows_of(lat_ap, ctx_ap, halo_ap, lat_sl, ctx_sl, halo_sl):
            def f(blk):
                if blk["kind"] == "lat":
                    return lat_ap[blk["t0"]:blk["t0"] + blk["n"], :], lat_sl
                if blk["kind"] == "ctx":
                    return ctx_ap, ctx_sl
                return halo_ap, halo_sl
            return f

        import os as _os
        _lim = int(_os.environ.get("MK_STOP_AFTER", "100000"))
        _cnt = [0]

        def _wrap(f):
            def g(*a, **k):
                _cnt[0] += 1
                if _cnt[0] <= _lim:
                    return f(*a, **k)
            return g
        mod_phase, halo_phase, norm_phase, qk_phase, v_phase, restb_phase, gates_phase, conv_phase, attn_phase, \
            merge_phase, resid_phase, up_phase = [_wrap(f) for f in (
                mod_phase, halo_phase, norm_phase, qk_phase, v_phase, restb_phase, gates_phase, conv_phase, attn_phase,
                merge_phase, resid_phase, up_phase)]
        _tcc = _wrap(T.cc)
        for l in range(L):
            last = l == L - 1
            allb = [ctx_block] + lat_blocks + [halo_block]
            main = lat_blocks if last else [ctx_block] + lat_blocks
            if l == 0:
                for l2 in range(L):
                    mod_phase(l2)
                weights_phase(0)
            if l == 0:
                src1 = rows_of(x_in, ctx_in, xh_in, [], [], [])
                xold1 = lambda blk: ((x_in[blk["t0"]:blk["t0"] + blk["n"], :], []) if blk["kind"] == "lat" else (ctx_in, []))
            else:
                halo_phase(xo, [sl("xo")])
                src1 = rows_of(xo[CTX:CTX + TL, :], xo[0:CTX, :], xhs, [sl("xo")], [sl("xo")], [sl("xhs")])
                xold1 = lambda blk: ((xo[CTX + blk["t0"]:CTX + blk["t0"] + blk["n"], :], [sl("xo")]) if blk["kind"] == "lat"
                                     else (xo[0:CTX, :], [sl("xo")]))
            norm_phase(l, "mix", src1, allb)
            qk_phase(l, False, [ctx_block] + lat_blocks)
            v_phase(l, [ctx_block] + lat_blocks)
            for ci in range(NKT // TPC):
                rr_ = TPC * P
                _tcc(G4, KT_loc[ci * rr_:(ci + 1) * rr_, :], KT_all[ci * 4 * rr_:(ci + 1) * 4 * rr_, :],
                     reads=[sl("KT_loc")], writes=[sl(("KT_all", ci))])
            for ci in range(TL // P):
                _tcc(G4, V_loc[ci * P:(ci + 1) * P, :], V_all[ci * 4 * P:(ci + 1) * 4 * P, :],
                     reads=[sl("V_loc")], writes=[sl(("V_all", ci))])
            if l + 1 < L:
                weights_phase(l + 1)
            qk_phase(l, True, main)
            restb_phase(l, main + [halo_block])
            gates_phase(l, main)
            conv_phase(l, main, "b")
            attn_phase(l, main)
            merge_phase(l, main)
            xm_of = lambda blk: ((xm[CTX + blk["t0"]:CTX + blk["t0"] + blk["n"], :], [sl("xm")]) if blk["kind"] == "lat"
                                 else (xm[0:CTX, :], [sl("xm")]))
            resid_phase(l, "w_out", D, mT, "mT", main, 2, xold1, xm_of)
            halo_phase(xm, [sl("xm")])
            src2 = rows_of(xm[CTX:CTX + TL, :], xm[0:CTX, :], xhs, [sl("xm")], [sl("xm")], [sl("xhs")])
            norm_phase(l, "ffn", src2, main + [halo_block])
            up_phase(l, main + [halo_block])
            conv_phase(l, main, "f")
            if last:
                xn_of = lambda blk: (y_out[blk["t0"]:blk["t0"] + blk["n"], :], [sl("y")])
            else:
                xn_of = lambda blk: ((xo[CTX + blk["t0"]:CTX + blk["t0"] + blk["n"], :], [sl("xo")]) if blk["kind"] == "lat"
                                     else (xo[0:CTX, :], [sl("xo")]))
            resid_phase(l, "w_down", DFF, aT, "aT", main, 5, xm_of, xn_of)
        T.barrier(full=True)
        T.replay()
    return nc


def make_in_maps(cfg, inp):
    D, TL, CTX, S = cfg.D, cfg.TL, cfg.CTX, cfg.S
    L = DEPTH
    f32 = np.float32
    x = np.asarray(inp["x"], f32)
    ctx = np.asarray(inp["ctx"], f32)
    c = np.asarray(inp["c"], f32)
    c_ctx = np.asarray(inp["c_ctx"], f32)
    cvec = np.stack([c[0], c[1], c_ctx], -1)
    D8, F8, B8 = D // 8, cfg.DFF // 8, 3 * cfg.BR // 8
    kc8 = D8 // P
    w_br = np.asarray(inp["w_branch"], f32).reshape(L, 3 * cfg.BR, D)
    ident = np.eye(P, dtype=f32)
    RT = np.zeros((P, P), f32)
    for d in range(P):
        if (d % 64) < 32:
            RT[d + 32, d] = -1.0
        else:
            RT[d - 32, d] = 1.0
    consts = np.stack([ident, RT, np.ones((P, P), f32)])
    quarter = P // 4
    freqs = (ROPE_THETA ** (-np.arange(quarter, dtype=np.float32) / quarter)).astype(f32)

    def rep(v):
        return np.ascontiguousarray(np.broadcast_to(np.asarray(v, f32)[:, None, :], (L, P, v.shape[-1])))

    gq = np.zeros((P, L * 3), f32)
    gk = np.zeros((P, L * 3), f32)
    for l in range(L):
        gq[:, l * 3 + 0] = inp["g_qa"][l]
        gq[:, l * 3 + 1] = inp["g_qc"][l, 0]
        gq[:, l * 3 + 2] = inp["g_qc"][l, 1]
        gk[:, l * 3 + 0] = inp["g_ka"][l]
        gk[:, l * 3 + 1] = inp["g_kc"][l, 0]
        gk[:, l * 3 + 2] = inp["g_kc"][l, 1]
    wcb = np.ascontiguousarray(np.asarray(inp["w_conv_b"], f32).reshape(L, 3, cfg.BW // P, P).transpose(3, 0, 1, 2))
    wcf = np.ascontiguousarray(np.asarray(inp["w_conv_ffn"], f32).reshape(L, 3, cfg.DFF // P, P).transpose(3, 0, 1, 2))
    lamq = np.ascontiguousarray(np.asarray(inp["lam_qk"], f32).reshape(L * 4, P).T)
    gsub = np.ascontiguousarray(np.asarray(inp["g_subln"], f32).reshape(L * 2, P).T)
    g_mix = rep(inp["g_norm_mix"])
    g_ffn = rep(inp["g_norm_ffn"])
    b_mod = np.ascontiguousarray(np.asarray(inp["b_mod"], f32)[:, None, :])
    maps = []
    for core in range(NCORE):
        b, q = core // 4, core % 4
        s, e = q * TL, (q + 1) * TL
        xh = np.zeros((2, D), f32)
        hm = np.zeros((P, 2), f32)
        if s > 0:
            xh[0] = x[b, s - 1]
            hm[:, 0] = 1.0
        if e < S:
            xh[1] = x[b, e]
            hm[:, 1] = 1.0
        selB = np.zeros((25, 2, P), f32)
        for r in range(8):
            selB[r * 3 + b, 0, :] = 1.0
            selB[r * 3 + 2, 1, :] = 1.0
        selB[24, :, :] = 1.0
        selH = np.zeros((8, 2), f32)
        if q > 0:
            selH[2 * (q - 1) + 1, 0] = 1.0
        if q < 3:
            selH[2 * (q + 1), 1] = 1.0
        pos = np.arange(s, e)
        row = (pos // GW).astype(f32)
        colp = (pos % GW).astype(f32)
        ang_r = row[:, None] * freqs
        ang_c = colp[:, None] * freqs
        ang = np.concatenate([ang_r, ang_r, ang_c, ang_c], -1).astype(f32)
        cT = np.ascontiguousarray(cvec[core * D8:(core + 1) * D8].reshape(kc8, P, 3).transpose(1, 0, 2))
        m = {
            "x": np.ascontiguousarray(x[b, s:e]), "xh": xh, "ctx": np.ascontiguousarray(ctx[b]), "cT": cT,
            "w_mod": np.ascontiguousarray(inp["w_mod"][:, core * D8:(core + 1) * D8, :]), "b_mod": b_mod,
            "selB": selB, "selH": selH, "hmask": hm, "g_mix": g_mix, "g_ffn": g_ffn, "gq": gq, "gk": gk,
            "wcb": wcb, "wcf": wcf, "lamq": lamq, "gsub": gsub,
            "cosT": np.ascontiguousarray(np.cos(ang).T.astype(f32)), "sinT": np.ascontiguousarray(np.sin(ang).T.astype(f32)),
            "consts": consts,
            "w_in": np.ascontiguousarray(inp["w_in"][:, shard_rows(D8, cfg.INC, b, q), :]),
            "w_br": np.ascontiguousarray(w_br[:, shard_rows(B8, D, b, q), :]),
            "w_out": np.ascontiguousarray(inp["w_out"][:, shard_rows(D8, D, b, q), :]),
            "w_up": np.ascontiguousarray(inp["w_up"][:, shard_rows(D8, 2 * cfg.DFF, b, q), :]),
            "w_down": np.ascontiguousarray(inp["w_down"][:, shard_rows(F8, D, b, q), :]),
        }
        maps.append(m)
    return maps


def run(cfg, inp):
    nc = build(cfg)
    maps = make_in_maps(cfg, inp)
    res = run_bass_kernel_spmd(nc, maps, core_ids=list(range(NCORE)))
    out = np.zeros((2, cfg.S, cfg.D), np.float32)
    for core in range(NCORE):
        b, q = core // 4, core % 4
        out[b, q * cfg.TL:(q + 1) * cfg.TL] = res.results[core]["y"]
    return out


def kernel(**inputs):
    inp = {k: np.asarray(v) for k, v in inputs.items()}
    return run(Cfg(), inp)
```

```python
import math
import os
from contextlib import ExitStack
import numpy as np
import concourse.bass as bass
import concourse.mybir as mybir
from concourse.bass_utils import run_bass_kernel_spmd

F32 = mybir.dt.float32
BF16 = mybir.dt.bfloat16
AF = mybir.ActivationFunctionType
ALU = mybir.AluOpType
P = 128
EPS = 1e-6
GW = 64
ROPE_THETA = 10000.0
NCORE = 8
DEPTH = 2


class Cfg:
    def __init__(s, D=4096, S=8192, CTX=256, AQH=16, AKVH=4, BW=2048, CH=8, DFF=8192):
        s.D, s.S, s.CTX, s.AQH, s.AKVH, s.BW, s.CH, s.DFF = D, S, CTX, AQH, AKVH, BW, CH, DFF
        s.TL = S // 4
        s.AQ = AQH * P
        s.AKV = AKVH * P
        s.CQK = CH * 2 * P
        s.CV = CH * 2 * P
        s.BR = s.AQ
        assert s.BW == s.BR and s.CV == s.BR
        s.KVC = 2 * s.AKV + s.CQK + s.CV
        s.REST = s.AQ + s.CQK + 3 * BW + 3 * D
        s.INC = s.KVC + s.REST
        s.G = AQH // AKVH
        s.KC = D // P
        s.NKT = AKVH + 2 * CH
        s.NQT = AQH + 2 * CH
        s.VC = s.AKV + s.CV
        s.NB = 512
        assert s.TL % 512 == 0 and CTX % P == 0 and CTX <= 512
        assert (D // 8) % P == 0 and (DFF // 8) % P == 0 and (3 * s.BR // 8) % P == 0


class Slot:
    __slots__ = ("w", "r")

    def __init__(s):
        s.w = {}
        s.r = {}


def _merge(dst, src):
    for k, (sem, v) in src.items():
        if k not in dst or dst[k][1] < v:
            dst[k] = (sem, v)


class Tracker:
    ND = 8

    def __init__(self, nc, es):
        self.nc = nc
        self.es = es
        self.names = ["pe", "act", "dve", "pool", "sp"]
        self.ops = {k: [] for k in self.names}
        self.msem = {k: es.enter_context(nc.semaphore("m_" + k)) for k in self.names}
        self.mcnt = {k: 0 for k in self.names}
        self.mold = []
        self.MAXC = int(os.environ.get("MK_MAXC", "60000"))
        self.dsem = {q: [es.enter_context(nc.semaphore("d_%s%d" % (q, i))) for i in range(self.ND)]
                     for q in ("sp", "pool", "act")}
        self.dcnt = {q: [0] * self.ND for q in ("sp", "pool", "act")}
        self.dnext = {"sp": 0, "pool": 0, "act": 0}
        self.NCC = 16
        self.ccs = [es.enter_context(nc.semaphore("cc%d" % i)) for i in range(self.NCC)]
        self.cccnt = [0] * self.NCC
        self.ccnext = 0
        self.seen = {k: {} for k in self.names}
        self.slots = {}

    def slot(self, key):
        s = self.slots.get(key)
        if s is None:
            s = self.slots[key] = Slot()
        return s

    def _emit(self, eng, fn, waits, inc):
        seen = self.seen[eng]
        wl = []
        for k, (sem, v) in waits.items():
            if seen.get(k, 0) < v:
                seen[k] = v
                wl.append((sem, v))
        self.ops[eng].append((wl, fn, inc))

    def _deps(self, reads, writes):
        waits = {}
        for s in reads:
            _merge(waits, s.w)
        for s in writes:
            _merge(waits, s.w)
            _merge(waits, s.r)
        return waits

    def _reg(self, tok, reads, writes):
        k = id(tok[0])
        for s in reads:
            s.r[k] = tok
        for s in writes:
            s.w = {k: tok}
            s.r = {}

    def _bump(self, eng):
        if self.mcnt[eng] >= self.MAXC:
            self.mold.append((self.msem[eng], self.mcnt[eng]))
            self.msem[eng] = self.es.enter_context(self.nc.semaphore("m_%s_%d" % (eng, len(self.mold))))
            self.mcnt[eng] = 0
        self.mcnt[eng] += 1

    def op(self, eng, meth, kw, reads=(), writes=()):
        fn = (meth, kw)
        waits = self._deps(reads, writes)
        self._bump(eng)
        tok = (self.msem[eng], self.mcnt[eng])
        self._emit(eng, fn, waits, (self.msem[eng], 1))
        self._reg(tok, reads, writes)
        return tok

    def mm_group(self, ps_slot, items, first=True, final=True, meth="matmul"):
        n = len(items)
        allr = []
        tok = None
        for i, (kw, rs) in enumerate(items):
            waits = {}
            for s in rs:
                _merge(waits, s.w)
            if i == 0 and first:
                _merge(waits, ps_slot.w)
                _merge(waits, ps_slot.r)
            allr += list(rs)
            if i < n - 1:
                self._emit("pe", (meth, kw), waits, None)
            else:
                self._bump("pe")
                tok = (self.msem["pe"], self.mcnt["pe"])
                self._emit("pe", (meth, kw), waits, (self.msem["pe"], 1))
        self._reg(tok, allr, [ps_slot] if final else [])
        return tok

    def dma(self, q, out, in_, reads=(), writes=()):
        i = self.dnext[q]
        self.dnext[q] = (i + 1) % self.ND
        sem = self.dsem[q][i]
        prev = self.dcnt[q][i]
        waits = self._deps(reads, writes)
        if prev:
            _merge(waits, {id(sem): (sem, prev)})
        self.dcnt[q][i] = prev + 16
        tok = (sem, prev + 16)
        self._emit(q, ("dma_start", dict(out=out, in_=in_, allow_slow_non_contiguous=True)), waits, (sem, 16))
        self._reg(tok, reads, writes)
        return tok

    def cc(self, groups, in_ap, out_ap, reads=(), writes=()):
        i = self.ccnext
        self.ccnext = (i + 1) % self.NCC
        sem = self.ccs[i]
        prev = self.cccnt[i]
        waits = self._deps(reads, writes)
        if prev:
            _merge(waits, {id(sem): (sem, prev)})
        self.cccnt[i] = prev + 1
        tok = (sem, prev + 1)
        self._emit("pool", ("_cc", dict(groups=groups, in_ap=in_ap, out_ap=out_ap)), waits, (sem, 1))
        self._reg(tok, reads, writes)
        return tok

    def barrier(self, full=False):
        toks = {}
        for k in self.names:
            if self.mcnt[k]:
                toks[id(self.msem[k])] = (self.msem[k], self.mcnt[k])
        for sem, v in self.mold:
            toks[id(sem)] = (sem, v)
        for q in (("sp", "pool", "act") if full else ("sp", "act")):
            for i in range(self.ND):
                if self.dcnt[q][i]:
                    toks[id(self.dsem[q][i])] = (self.dsem[q][i], self.dcnt[q][i])
        if full:
            for i, sem in enumerate(self.ccs):
                if self.cccnt[i]:
                    toks[id(sem)] = (sem, self.cccnt[i])
        for k in self.names:
            self._emit(k, None, dict(toks), None)

    def replay(self):
        nc = self.nc
        engs = {"pe": "tensor", "act": "scalar", "dve": "vector", "pool": "gpsimd", "sp": "sync"}
        with nc.Block() as block:
            for k in self.names:
                ops = self.ops[k]

                def run(e, ops=ops):
                    for wl, fn, inc in ops:
                        for sem, v in wl:
                            e.wait_ge(sem, v)
                        if fn is not None:
                            meth, kw = fn
                            if meth == "_cc":
                                ins = e.collective_compute("AllGather", ALU.bypass, replica_groups=kw["groups"],
                                                           ins=[kw["in_ap"]], outs=[kw["out_ap"]])
                            else:
                                ins = getattr(e, meth)(**kw)
                            if inc is not None:
                                ins.then_inc(inc[0], inc[1])
                getattr(block, engs[k])(run)


_UID = [0]


class Ring:
    def __init__(self, T, es, nc, name, shape, dt, n, psum=False):
        alloc = nc.psum_tensor if psum else nc.sbuf_tensor
        _UID[0] += 1
        self.bufs = [es.enter_context(alloc("r_%s_%d_%d" % (name, _UID[0], i), list(shape), dt)) for i in range(n)]
        self.slots = [Slot() for _ in range(n)]
        self.i = 0
        self.n = n

    def next(self):
        b, s = self.bufs[self.i], self.slots[self.i]
        self.i = (self.i + 1) % self.n
        return b, s


def _largest_div(n, cap, mult=1):
    best = None
    for d in range(1, n + 1):
        if n % d == 0 and d <= cap and d % mult == 0:
            best = d
    assert best is not None, (n, cap, mult)
    return best


def wchunks(R, C):
    r1 = _largest_div(R, (1 << 20) // (2 * C))
    m2 = _largest_div(4 * R, (2 << 20) // (2 * C))
    return r1, m2


def shard_rows(R, C, grp, rank):
    r1, m2 = wchunks(R, C)
    s_ = np.arange(R)
    i, t = s_ // r1, s_ % r1
    a = i * 4 * r1 + rank * r1 + t
    return (a // m2) * 2 * m2 + grp * m2 + (a % m2)


def _pow2div(c, cap=2048):
    b = cap
    while c % b:
        b //= 2
    return b


def build(cfg):
    nc = bass.Bass("TRN2", target_bir_lowering=False)
    D, TL, CTX, KC = cfg.D, cfg.TL, cfg.CTX, cfg.KC
    BW, BR, DFF, INC = cfg.BW, cfg.BR, cfg.DFF, cfg.INC
    AKV, CQK, CV, AQ, KVC, VC = cfg.AKV, cfg.CQK, cfg.CV, cfg.AQ, cfg.KVC, cfg.VC
    NKT, NQT, CH, AQH, AKVH, G = cfg.NKT, cfg.NQT, cfg.CH, cfg.AQH, cfg.AKVH, cfg.G
    L = DEPTH
    TT = CTX + TL + 2
    TA = CTX + TL
    D8, F8, B8 = D // 8, DFF // 8, 3 * BR // 8
    kc8 = D8 // P
    NKEY = CTX + 4 * TL
    NKEYT = NKEY // P
    SCALE = float(P) ** -0.5
    TPC = _largest_div(NKT, (1 << 20) // (2 * TL * P))
    assert VC * P * 2 <= (1 << 20)

    def din(name, shape, dt=F32):
        return nc.dram_tensor(name, list(shape), dt, kind="ExternalInput").ap()

    def dsc(name, shape, dt=BF16):
        return nc.dram_tensor(name, list(shape), dt, kind="Internal").ap()

    x_in = din("x", [TL, D])
    xh_in = din("xh", [2, D])
    ctx_in = din("ctx", [CTX, D])
    cT_in = din("cT", [P, kc8, 3])
    wmod_in = din("w_mod", [L, D8, 6 * D])
    bmod_in = din("b_mod", [L, 1, 6 * D])
    selB_in = din("selB", [25, 2, P])
    selH_in = din("selH", [8, 2])
    hmask_in = din("hmask", [P, 2])
    gmix_in = din("g_mix", [L, P, D])
    gffn_in = din("g_ffn", [L, P, D])
    gq_in = din("gq", [P, L * 3])
    gk_in = din("gk", [P, L * 3])
    wcb_in = din("wcb", [P, L, 3, BW // P])
    wcf_in = din("wcf", [P, L, 3, DFF // P])
    lamq_in = din("lamq", [P, L * 4])
    gsub_in = din("gsub", [P, L * 2])
    cos_in = din("cosT", [P, TL])
    sin_in = din("sinT", [P, TL])
    cst_in = din("consts", [3, P, P])
    win_in = din("w_in", [L, D8, INC])
    wbr_in = din("w_br", [L, B8, D])
    wout_in = din("w_out", [L, D8, D])
    wup_in = din("w_up", [L, D8, 2 * DFF])
    wdn_in = din("w_down", [L, F8, D])
    y_out = nc.dram_tensor("y", [TL, D], F32, kind="ExternalOutput").ap()

    xm = dsc("xm", [TA, D], F32)
    xo = dsc("xo", [TA, D], F32)
    xhs = dsc("xhs", [2, D], F32)
    hT = dsc("hT", [D, TT])
    KT_loc = dsc("KT_loc", [NKT * P, TL])
    KT_all = dsc("KT_all", [4 * NKT * P, TL])
    KT_ctx = dsc("KT_ctx", [NKT * P, CTX])
    V_loc = dsc("V_loc", [TL, VC])
    V_all = dsc("V_all", [4 * TL, VC])
    V_ctx = dsc("V_ctx", [CTX, VC])
    QT = dsc("QT", [NQT * P, TA])
    bbT = dsc("bbT", [BW, TA])
    zT = dsc("zT", [BW, TL + 2], F32)
    zTc = dsc("zTc", [BW, CTX + 2], F32)
    sgT = dsc("sgT", [3 * D, TA])
    yT = dsc("yT", [3 * BR, TA])
    mT = dsc("mT", [D, TA])
    uT = dsc("uT", [DFF, TA])
    gT = dsc("gT", [DFF, TL + 2], F32)
    gTc = dsc("gTc", [DFF, CTX + 2], F32)
    aT = dsc("aT", [DFF, TA])
    modpart = dsc("modpart", [3, 6 * D], F32)
    modmid = dsc("modmid", [12, 6 * D], F32)
    modall = dsc("modall", [24, 6 * D], F32)
    modbc = dsc("modbc", [L * 12, P, D], F32)
    hb_in = dsc("hb_in", [2, D], F32)
    hb_all = dsc("hb_all", [8, D], F32)
    wspec = [("w_in", win_in, D8, INC), ("w_br", wbr_in, B8, D), ("w_out", wout_in, D8, D),
             ("w_up", wup_in, D8, 2 * DFF), ("w_down", wdn_in, F8, D)]
    Wb = {}
    for l in range(L):
        for nm, _, R, C in wspec:
            Wb[(nm, l)] = (dsc("%s_s%d" % (nm, l), [R, C]), dsc("%s_m%d" % (nm, l), [4 * R, C]),
                           dsc("%s_f%d" % (nm, l), [8 * R, C]))

    G4 = [[0, 1, 2, 3], [4, 5, 6, 7]]
    G2 = [[0, 4], [1, 5], [2, 6], [3, 7]]

    es = ExitStack()
    with es:
        T = Tracker(nc, es)
        sl = T.slot

        def sb(st, name, shape, dt=F32):
            _UID[0] += 1
            return st.enter_context(nc.sbuf_tensor("sb_%s_%d" % (name, _UID[0]), list(shape), dt))

        def ps(st, name, shape=(P, 512), dt=F32):
            _UID[0] += 1
            return st.enter_context(nc.psum_tensor("ps_%s_%d" % (name, _UID[0]), list(shape), dt))

        cbf = sb(es, "cbf", [P, 3, P], BF16)
        ones32 = sb(es, "ones32", [P, P], F32)
        gq = sb(es, "gq", [P, L * 3])
        gk = sb(es, "gk", [P, L * 3])
        wcb = sb(es, "wcb", [P, L, 3, BW // P])
        wcf = sb(es, "wcf", [P, L, 3, DFF // P])
        lamq = sb(es, "lamq", [P, L * 4])
        gsub = sb(es, "gsub", [P, L * 2])
        hmask = sb(es, "hmask", [P, 2])
        selH = sb(es, "selH", [8, 2])
        selB = sb(es, "selB", [25, 2, P])
        zeros = sb(es, "zeros", [P, 64])
        nlam = sb(es, "nlam", [P, L])
        gsc = sb(es, "gsc", [P, L * 2])
        S_c = sl("consts")
        T.dma("pool", cbf[:, :, :], cst_in.rearrange("c p m -> p c m"), writes=[S_c])
        T.dma("sp", ones32[:, :], cst_in[2], writes=[S_c])
        for t_, a_ in ((gq, gq_in), (gk, gk_in), (lamq, lamq_in), (gsub, gsub_in), (hmask, hmask_in),
                       (selH, selH_in)):
            T.dma("sp", t_[:, :], a_, writes=[S_c])
        T.dma("sp", wcb[:, :, :, :], wcb_in, writes=[S_c])
        T.dma("sp", wcf[:, :, :, :], wcf_in, writes=[S_c])
        T.dma("sp", selB[:, :, :], selB_in, writes=[S_c])
        T.op("dve", "memset", dict(ap=zeros[:, :], constant=0.0), writes=[S_c])
        T.dma("sp", zTc.rearrange("(c p) t -> p c t", p=P)[:, :, 0:1], zeros[:, 0:BW // P].unsqueeze(2),
              reads=[S_c], writes=[sl("zTc")])
        T.dma("sp", zTc.rearrange("(c p) t -> p c t", p=P)[:, :, CTX + 1:CTX + 2],
              zeros[:, 0:BW // P].unsqueeze(2), reads=[S_c], writes=[sl("zTc")])
        T.dma("sp", gTc.rearrange("(c p) t -> p c t", p=P)[:, :, 0:1], zeros[:, 0:DFF // P].unsqueeze(2),
              reads=[S_c], writes=[sl("gTc")])
        T.dma("sp", gTc.rearrange("(c p) t -> p c t", p=P)[:, :, CTX + 1:CTX + 2],
              zeros[:, 0:DFF // P].unsqueeze(2), reads=[S_c], writes=[sl("gTc")])

        with ExitStack() as st:
            pr = sb(st, "lam_pr", [P, 2 * L])
            ex = sb(st, "lam_ex", [P, 2 * L])
            pl = ps(st, "lam_ps", [P, 2 * L])
            s_pr, s_ex, s_pl = Slot(), Slot(), Slot()
            lq4 = lamq[:, :].rearrange("p (l f) -> p l f", f=4)
            pr3 = pr[:, :].rearrange("p (l f) -> p l f", f=2)
            T.op("dve", "tensor_tensor", dict(out=pr3[:, :, 0], in0=lq4[:, :, 0], in1=lq4[:, :, 1], op=ALU.mult),
                 reads=[S_c], writes=[s_pr])
            T.op("dve", "tensor_tensor", dict(out=pr3[:, :, 1], in0=lq4[:, :, 2], in1=lq4[:, :, 3], op=ALU.mult),
                 reads=[S_c], writes=[s_pr])
            T.mm_group(s_pl, [(dict(out=pl[:, :], lhsT=ones32[:, :], rhs=pr[:, :], start=True, stop=True),
                               [s_pr, S_c])])
            T.op("act", "activation", dict(out=ex[:, :], in_=pl[:, :], func=AF.Exp), reads=[s_pl], writes=[s_ex])
            for l in range(L):
                li = 0.8 - 0.6 * math.exp(-0.3 * l)
                T.op("dve", "tensor_tensor", dict(out=nlam[:, l:l + 1], in0=ex[:, 2 * l + 1:2 * l + 2],
                                                           in1=ex[:, 2 * l:2 * l + 1], op=ALU.subtract),
                     reads=[s_ex], writes=[S_c])
                T.op("dve", "tensor_scalar", dict(out=nlam[:, l:l + 1], in0=nlam[:, l:l + 1],
                                                                  scalar1=-li, scalar2=None, op0=ALU.add),
                     reads=[S_c], writes=[S_c])
                T.op("dve", "tensor_scalar", dict(out=gsc[:, 2 * l:2 * l + 2], in0=gsub[:, 2 * l:2 * l + 2],
                                                                  scalar1=1.0 - li, scalar2=None, op0=ALU.mult),
                     reads=[S_c], writes=[S_c])
            T.barrier()

        Wslots = {}

        def weights_phase(l):
            for nm, src, R, C in wspec:
                sh, mid, full = Wb[(nm, l)]
                b = _pow2div(C)
                r1, m2 = wchunks(R, C)
                S_sh = sl((nm, l, "s"))
                T.dma("pool", sh.rearrange("r (a b) -> (r a) b", b=b), src[l].rearrange("r (a b) -> (r a) b", b=b),
                      writes=[S_sh])
                n1 = R // r1
                for i in range(n1):
                    T.cc(G4, sh[i * r1:(i + 1) * r1, :], mid[i * 4 * r1:(i + 1) * 4 * r1, :], reads=[S_sh],
                         writes=[sl((nm, l, "m", i))])
                n2 = 4 * R // m2
                fs = []
                for j in range(n2):
                    i0, i1 = (j * m2) // (4 * r1), ((j + 1) * m2 - 1) // (4 * r1)
                    fsl = sl((nm, l, "f", j))
                    T.cc(G2, mid[j * m2:(j + 1) * m2, :], full[j * 2 * m2:(j + 1) * 2 * m2, :],
                         reads=[sl((nm, l, "m", i)) for i in range(i0, i1 + 1)], writes=[fsl])
                    fs.append(fsl)
                Wslots[(nm, l)] = fs


        lat_blocks = [dict(kind="lat", t0=i * 512, n=512, col=CTX + i * 512) for i in range(TL // 512)]
        ctx_block = dict(kind="ctx", t0=0, n=CTX, col=0)
        halo_block = dict(kind="halo", t0=0, n=2, col=CTX + TL)

        def mod_phase(l):
            with ExitStack() as st:
                sT = sb(st, "sT", [P, kc8, 3])
                wm = Ring(T, st, nc, "wm", [P, kc8, 512], F32, 2)
                pp = Ring(T, st, nc, "modps", [P, 512], F32, 2, psum=True)
                stg = Ring(T, st, nc, "modst", [P, 512], F32, 2)
                s_sT = Slot()
                T.dma("sp", sT[:, :, :], cT_in, writes=[s_sT])
                T.op("act", "activation", dict(out=sT[:, :, :], in_=sT[:, :, :], func=AF.Silu),
                     reads=[], writes=[s_sT])
                S_mp = sl("modpart")
                for cg in range(6 * D // 512):
                    wt, ws = wm.next()
                    T.dma("sp", wt[:, :, :], wmod_in[l].rearrange("(k p) c -> p k c", p=P)[:, :, cg * 512:(cg + 1) * 512],
                          writes=[ws])
                    pt, pss = pp.next()
                    T.mm_group(pss, [(dict(out=pt[0:3, :], lhsT=sT[:, k, :], rhs=wt[:, k, :],
                                                                             start=(k == 0), stop=(k == kc8 - 1)),
                                      [s_sT, ws]) for k in range(kc8)])
                    so, sos = stg.next()
                    T.op("act", "activation", dict(out=so[0:3, :], in_=pt[0:3, :], func=AF.Copy),
                         reads=[pss], writes=[sos])
                    T.dma("act", modpart[:, cg * 512:(cg + 1) * 512], so[0:3, :], reads=[sos], writes=[S_mp])
                T.cc(G4, modpart, modmid, reads=[S_mp], writes=[sl("modmid")])
                T.cc(G2, modmid, modall, reads=[sl("modmid")], writes=[sl("modall")])
                gm = sb(st, "gmix", [P, D])
                gf = sb(st, "gffn", [P, D])
                s_g = Slot()
                T.dma("sp", gm[:, :], gmix_in[l], writes=[s_g])
                T.dma("sp", gf[:, :], gffn_in[l], writes=[s_g])
                ge = Ring(T, st, nc, "ge", [25, 512], F32, 2)
                for sec in range(6):
                    for ch in range(D // 512):
                        c0 = sec * D + ch * 512
                        gt, gs = ge.next()
                        T.dma("sp", gt[0:24, :], modall[:, c0:c0 + 512], reads=[sl("modall")], writes=[gs])
                        T.dma("sp", gt[24:25, :], bmod_in[l][:, c0:c0 + 512], writes=[gs])
                        for v in range(2):
                            pt, pss = pp.next()
                            T.mm_group(pss, [(dict(out=pt[:, :], lhsT=selB[:, v, :], rhs=gt[:, :],
                                                                                     start=True, stop=True), [gs, S_c])])
                            so, sos = stg.next()
                            if sec in (1, 4):
                                gg = gm if sec == 1 else gf
                                T.op("dve", "scalar_tensor_tensor", dict(
                                    out=so[:, :], in0=pt[:, :], scalar=1.0, in1=gg[:, ch * 512:(ch + 1) * 512],
                                    op0=ALU.add, op1=ALU.mult), reads=[pss, s_g], writes=[sos])
                            else:
                                T.op("act", "activation", dict(out=so[:, :], in_=pt[:, :], func=AF.Copy),
                                     reads=[pss], writes=[sos])
                            T.dma("act", modbc[l * 12 + v * 6 + sec][:, ch * 512:(ch + 1) * 512], so[:, :], reads=[sos],
                                  writes=[sl(("modbc", l * 12 + v * 6 + sec))])
                T.barrier()

        def norm_phase(l, which, src_of, blocks):
            secA, secB = (1, 0) if which == "mix" else (4, 3)
            with ExitStack() as st:
                A = sb(st, "nA", [P, D])
                Bt = sb(st, "nB", [P, D])
                s_ab = Slot()
                xr = Ring(T, st, nc, "nx", [P, D], F32, 2)
                tmp = sb(st, "ntmp", [P, D])
                hb = sb(st, "nhb", [P, D], BF16)
                junk = sb(st, "njunk", [P, D], BF16)
                st4 = Ring(T, st, nc, "nst", [P, 8], F32, 2)
                hTs = Ring(T, st, nc, "nhTs", [P, KC, 512], BF16, 1)
                pTr = Ring(T, st, nc, "npT", [P, KC, P], BF16, 2, psum=True)
                s_tmp, s_hb, s_junk = Slot(), Slot(), Slot()
                cur_v = None
                for blk in blocks:
                    v = 1 if blk["kind"] == "ctx" else 0
                    if v != cur_v:
                        T.dma("sp", A[:, :], modbc[l * 12 + v * 6 + secA], reads=[sl(("modbc", l * 12 + v * 6 + secA))], writes=[s_ab])
                        T.dma("sp", Bt[:, :], modbc[l * 12 + v * 6 + secB], reads=[sl(("modbc", l * 12 + v * 6 + secB))], writes=[s_ab])
                        cur_v = v
                    src = src_of(blk)
                    n = blk["n"]
                    ht, hts = hTs.next()
                    for sub in range((n + P - 1) // P):
                        r = min(P, n - sub * P)
                        xt, xs = xr.next()
                        T.dma("sp", xt[0:r, :], src[0][sub * P:sub * P + r, :], reads=src[1], writes=[xs])
                        s4, s4s = st4.next()
                        T.op("act", "activation", dict(out=junk[0:r, :], in_=xt[0:r, :], func=AF.Square,
                                                                              accum_out=s4[0:r, 0:1]),
                             reads=[xs], writes=[s_junk, s4s])
                        T.op("dve", "tensor_scalar", dict(out=s4[0:r, 1:2], in0=s4[0:r, 0:1], scalar1=1.0 / D,
                                                                          scalar2=EPS, op0=ALU.mult, op1=ALU.add),
                             reads=[], writes=[s4s])
                        T.op("act", "activation", dict(out=s4[0:r, 2:3], in_=s4[0:r, 1:2], func=AF.Sqrt),
                             reads=[], writes=[s4s])
                        T.op("dve", "reciprocal", dict(out=s4[0:r, 3:4], in_=s4[0:r, 2:3]),
                             reads=[], writes=[s4s])
                        T.op("dve", "scalar_tensor_tensor", dict(
                            out=tmp[0:r, :], in0=xt[0:r, :], scalar=s4[0:r, 3:4], in1=A[0:r, :], op0=ALU.mult, op1=ALU.mult),
                            reads=[xs, s4s, s_ab], writes=[s_tmp])
                        T.op("dve", "tensor_tensor", dict(out=hb[0:r, :], in0=tmp[0:r, :], in1=Bt[0:r, :], op=ALU.add),
                             reads=[s_tmp, s_ab], writes=[s_hb])
                        pT, pTs = pTr.next()
                        T.mm_group(pTs, [(dict(out=pT[:, k, 0:r], in_=hb[0:r, k * P:(k + 1) * P],
                                                                                 identity=cbf[0:r, 0, 0:r]), [s_hb, S_c])
                                         for k in range(KC)], meth="transpose")
                        T.op("act", "activation", dict(
                            out=ht[:, :, sub * P:sub * P + r], in_=pT[:, :, 0:r], func=AF.Copy), reads=[pTs], writes=[hts])
                    T.dma("act", hT.rearrange("(k p) t -> p k t", p=P)[:, :, blk["col"]:blk["col"] + n], ht[:, :, 0:n],
                          reads=[hts], writes=[sl(("hT", blk["col"]))])
                T.barrier()

        def proj_fm(Wfull, S_w, Krows, actT, act_slot_of, blocks, groups, epilogue, k0=0):
            kc = Krows // P
            with ExitStack() as st:
                wr = Ring(T, st, nc, "pfw", [P, kc, 512], BF16, 3)
                ar = Ring(T, st, nc, "pfa", [P, kc, 512], BF16, 1)
                pr_ = Ring(T, st, nc, "pfps", [P, 512], F32, 4, psum=True)
                ctxo = epilogue("init", st)
                for blk in blocks:
                    n = blk["n"]
                    at, as_ = ar.next()
                    T.dma("sp", at[:, :, 0:n], actT.rearrange("(k p) t -> p k t", p=P)[:, k0:k0 + kc, blk["col"]:blk["col"] + n],
                          reads=[act_slot_of(blk)], writes=[as_])
                    for c0, tags in groups(blk):
                        wt, ws = wr.next()
                        w = len(tags) * P
                        T.dma("sp", wt[:, :, 0:w], Wfull.rearrange("(k p) c -> p k c", p=P)[:, k0:k0 + kc, c0:c0 + w],
                              reads=list(S_w), writes=[ws])
                        for ti, tag in enumerate(tags):
                            pt, pss = pr_.next()
                            T.mm_group(pss, [(dict(out=pt[:, 0:n], lhsT=wt[:, k, ti * P:(ti + 1) * P], rhs=at[:, k, 0:n],
                                start=(k == 0), stop=(k == kc - 1)), [ws, as_]) for k in range(kc)])
                            epilogue(ctxo, blk, tag, pt, pss)
                T.barrier()

        def proj_tm(Wfull, S_w, Krows, actT, act_slot_of, blocks, colgroups, epilogue):
            kc = Krows // P
            nslab = (kc + 31) // 32
            with ExitStack() as st:
                wr = Ring(T, st, nc, "ptw", [P, 32, 512], BF16, 2)
                at = sb(st, "pta", [P, kc, 512], BF16)
                as_ = Slot()
                pr_ = Ring(T, st, nc, "ptps", [P, 512], F32, 8, psum=True)
                ctxo = epilogue("init", st)
                for blk in blocks:
                    n = blk["n"]
                    nsub = (n + P - 1) // P
                    T.dma("sp", at[:, :, 0:n], actT.rearrange("(k p) t -> p k t", p=P)[:, :, blk["col"]:blk["col"] + n],
                          reads=[act_slot_of(blk)], writes=[as_])
                    for wc0, oc0, w in colgroups:
                        accs = [pr_.next() for _ in range(nsub)]
                        pend = [[] for _ in range(nsub)]
                        for s_i in range(nslab):
                            ks = min(32, kc - s_i * 32)
                            wt, ws = wr.next()
                            T.dma("sp", wt[:, 0:ks, 0:w],
                                  Wfull.rearrange("(k p) c -> p k c", p=P)[:, s_i * 32:s_i * 32 + ks, wc0:wc0 + w],
                                  reads=list(S_w), writes=[ws])
                            for sub in range(nsub):
                                r = min(P, n - sub * P)
                                pt = accs[sub][0]
                                for k in range(ks):
                                    kk = s_i * 32 + k
                                    pend[sub].append((dict(out=pt[0:r, 0:w], lhsT=at[:, kk, sub * P:sub * P + r], rhs=wt[:, k, 0:w],
                                        start=(kk == 0), stop=(kk == kc - 1)), [ws, as_]))
                            last_slab = (s_i == nslab - 1)
                            for sub in range(nsub):
                                items = pend[sub]
                                pend[sub] = []
                                T.mm_group(accs[sub][1], items, first=(s_i == 0), final=last_slab)
                                if last_slab:
                                    r = min(P, n - sub * P)
                                    epilogue(ctxo, blk, sub, r, (wc0, oc0, w), accs[sub][0], accs[sub][1])
                T.barrier()

        def hT_slot(blk):
            return sl(("hT", blk["col"]))

        def wslice(l, nm):
            return Wb[(nm, l)][2], Wslots[(nm, l)]

        def qk_phase(l, is_q, blocks):
            Wf, S_w = wslice(l, "w_in")
            if is_q:
                base = KVC
                tiles = [(base + i * P, ("q", i, 0)) for i in range(AQH)] + \
                        [(base + AQ + i * P, ("q", AQH + i, 1 + (i % 2))) for i in range(2 * CH)]
                gtile = gq
            else:
                tiles = [(i * P, ("k", i, 0)) for i in range(AKVH)] + \
                        [(2 * AKV + i * P, ("k", AKVH + i, 1 + (i % 2))) for i in range(2 * CH)]
                gtile = gk

            def groups(blk):
                out = []
                i = 0
                while i < len(tiles):
                    j = i
                    while j + 1 < len(tiles) and j + 1 - i < 4 and tiles[j + 1][0] == tiles[j][0] + P:
                        j += 1
                    out.append((tiles[i][0], [t[1] for t in tiles[i:j + 1]]))
                    i = j + 1
                return out

            def epi(c, blk=None, tag=None, pt=None, pss=None):
                if c == "init":
                    st = blk
                    o = dict(
                        sq=Ring(T, st, nc, "qsq", [P, 512], BF16, 2), ssp=Ring(T, st, nc, "qssp", [P, 512], F32, 2, psum=True),
                        ln=Ring(T, st, nc, "qln", [P, 512], F32, 2), rr=Ring(T, st, nc, "qrr", [P, 512], F32, 2),
                        kn=Ring(T, st, nc, "qkn", [P, 512], BF16, 2), rot=Ring(T, st, nc, "qrot", [P, 512], F32, 2, psum=True),
                        t1=Ring(T, st, nc, "qt1", [P, 512], F32, 2), t2=Ring(T, st, nc, "qt2", [P, 512], F32, 2),
                        ko=Ring(T, st, nc, "qko", [P, 512], BF16, 3),
                        cos=sb(st, "qcos", [P, 512]), sin=sb(st, "qsin", [P, 512]), s_cs=Slot(), cs_t0=[None])
                    return o
                n = blk["n"]
                _, ti, gi = tag
                rope = blk["kind"] == "lat"
                sq, sqs = c["sq"].next()
                T.op("act", "activation", dict(out=sq[:, 0:n], in_=pt[:, 0:n], func=AF.Square), reads=[pss], writes=[sqs])
                sp_, sps = c["ssp"].next()
                T.mm_group(sps, [(dict(out=sp_[:, 0:n], lhsT=cbf[:, 2, :], rhs=sq[:, 0:n], start=True, stop=True),
                                  [sqs, S_c])])
                ln, lns = c["ln"].next()
                T.op("act", "activation", dict(out=ln[:, 0:n], in_=sp_[:, 0:n], func=AF.Ln, scale=1.0 / P, bias=epsb[:, 0:1]),
                     reads=[sps, S_c], writes=[lns])
                rr, rrs = c["rr"].next()
                T.op("act", "activation", dict(out=rr[:, 0:n], in_=ln[:, 0:n], func=AF.Exp, scale=-0.5), reads=[lns], writes=[rrs])
                col = l * 3 + gi
                if not rope:
                    ko, kos = c["ko"].next()
                    T.op("dve", "scalar_tensor_tensor", dict(out=ko[:, 0:n], in0=pt[:, 0:n], scalar=gtile[:, col:col + 1],
                                                                 in1=rr[:, 0:n], op0=ALU.mult, op1=ALU.mult),
                         reads=[pss, rrs, S_c], writes=[kos])
                else:
                    if c["cs_t0"][0] != blk["t0"]:
                        T.dma("sp", c["cos"][:, 0:n], cos_in[:, blk["t0"]:blk["t0"] + n], writes=[c["s_cs"]])
                        T.dma("sp", c["sin"][:, 0:n], sin_in[:, blk["t0"]:blk["t0"] + n], writes=[c["s_cs"]])
                        c["cs_t0"][0] = blk["t0"]
                    kn, kns = c["kn"].next()
                    T.op("dve", "scalar_tensor_tensor", dict(out=kn[:, 0:n], in0=pt[:, 0:n], scalar=gtile[:, col:col + 1],
                                                                 in1=rr[:, 0:n], op0=ALU.mult, op1=ALU.mult),
                         reads=[pss, rrs, S_c], writes=[kns])
                    rp, rps = c["rot"].next()
                    T.mm_group(rps, [(dict(out=rp[:, 0:n], lhsT=cbf[:, 1, :], rhs=kn[:, 0:n], start=True, stop=True),
                                      [kns, S_c])])
                    t1, t1s = c["t1"].next()
                    T.op("dve", "tensor_tensor", dict(out=t1[:, 0:n], in0=kn[:, 0:n], in1=c["cos"][:, 0:n], op=ALU.mult),
                         reads=[kns, c["s_cs"]], writes=[t1s])
                    t2, t2s = c["t2"].next()
                    T.op("dve", "tensor_tensor", dict(out=t2[:, 0:n], in0=rp[:, 0:n], in1=c["sin"][:, 0:n], op=ALU.mult),
                         reads=[rps, c["s_cs"]], writes=[t2s])
                    ko, kos = c["ko"].next()
                    T.op("dve", "tensor_tensor", dict(out=ko[:, 0:n], in0=t1[:, 0:n], in1=t2[:, 0:n], op=ALU.add),
                         reads=[t1s, t2s], writes=[kos])
                if is_q:
                    dst, dsl = QT[ti * P:(ti + 1) * P, blk["col"]:blk["col"] + n], sl(("QT", blk["col"]))
                elif blk["kind"] == "ctx":
                    dst, dsl = KT_ctx[ti * P:(ti + 1) * P, 0:n], sl("KT_ctx")
                else:
                    dst, dsl = KT_loc[ti * P:(ti + 1) * P, blk["t0"]:blk["t0"] + n], sl("KT_loc")
                T.dma("act", dst, ko[:, 0:n], reads=[kos], writes=[dsl])

            proj_fm(Wf, S_w, D, hT, hT_slot, blocks, groups, epi)

        def v_phase(l, blocks):
            Wf, S_w = wslice(l, "w_in")
            cgs = []
            for (w0, o0, tot) in ((AKV, 0, AKV), (2 * AKV + CQK, AKV, CV)):
                c = 0
                while c < tot:
                    w = min(512, tot - c)
                    cgs.append((w0 + c, o0 + c, w))
                    c += w

            def epi(c, blk=None, sub=None, r=None, cg=None, pt=None, pss=None):
                if c == "init":
                    return dict(vo=Ring(T, blk, nc, "vvo", [P, 512], BF16, 3))
                wc0, oc0, w = cg
                vo, vos = c["vo"].next()
                T.op("act", "activation", dict(out=vo[0:r, 0:w], in_=pt[0:r, 0:w], func=AF.Copy), reads=[pss], writes=[vos])
                if blk["kind"] == "ctx":
                    dst, dsl = V_ctx[sub * P:sub * P + r, oc0:oc0 + w], sl("V_ctx")
                else:
                    dst, dsl = V_loc[blk["t0"] + sub * P:blk["t0"] + sub * P + r, oc0:oc0 + w], sl("V_loc")
                T.dma("act", dst, vo[0:r, 0:w], reads=[vos], writes=[dsl])

            proj_tm(Wf, S_w, D, hT, hT_slot, blocks, cgs, epi)

        def restb_phase(l, blocks):
            Wf, S_w = wslice(l, "w_in")
            base = KVC + AQ + CQK
            nct = BW // P

            def groups(blk):
                out = []
                for c in range(0, nct, 4):
                    m = min(4, nct - c)
                    if blk["kind"] != "halo":
                        out.append((base + c * P, [("bb", c + i) for i in range(m)]))
                    out.append((base + BW + c * P, [("bc", c + i) for i in range(m)]))
                    out.append((base + 2 * BW + c * P, [("bx", c + i) for i in range(m)]))
                return out

            def epi(c, blk=None, tag=None, pt=None, pss=None):
                if c == "init":
                    st = blk
                    return dict(bbo=Ring(T, st, nc, "rbb", [P, 512], BF16, 3), bc=Ring(T, st, nc, "rbc", [P, 512], F32, 5),
                                z=Ring(T, st, nc, "rz", [P, 512], F32, 3), held={})
                n = blk["n"]
                kind, ci = tag
                if kind == "bb":
                    o, os_ = c["bbo"].next()
                    T.op("act", "activation", dict(out=o[:, 0:n], in_=pt[:, 0:n], func=AF.Copy), reads=[pss], writes=[os_])
                    T.dma("act", bbT[ci * P:(ci + 1) * P, blk["col"]:blk["col"] + n], o[:, 0:n], reads=[os_],
                          writes=[sl(("bbT", blk["col"]))])
                elif kind == "bc":
                    o, os_ = c["bc"].next()
                    T.op("act", "activation", dict(out=o[:, 0:n], in_=pt[:, 0:n], func=AF.Copy), reads=[pss], writes=[os_])
                    c["held"][ci] = (o, os_)
                else:
                    bc_, bcs = c["held"].pop(ci)
                    z, zs = c["z"].next()
                    T.op("dve", "tensor_tensor", dict(out=z[:, 0:n], in0=pt[:, 0:n], in1=bc_[:, 0:n], op=ALU.mult),
                         reads=[pss, bcs], writes=[zs])
                    rows = slice(ci * P, (ci + 1) * P)
                    if blk["kind"] == "halo":
                        T.op("dve", "tensor_tensor", dict(out=z[:, 0:2], in0=z[:, 0:2], in1=hmask[:, 0:2], op=ALU.mult),
                             reads=[S_c], writes=[zs])
                        T.dma("act", zT[rows, 0:1], z[:, 0:1], reads=[zs], writes=[sl("zT")])
                        T.dma("act", zT[rows, TL + 1:TL + 2], z[:, 1:2], reads=[zs], writes=[sl("zT")])
                    elif blk["kind"] == "ctx":
                        T.dma("act", zTc[rows, 1:1 + n], z[:, 0:n], reads=[zs], writes=[sl("zTc")])
                    else:
                        T.dma("act", zT[rows, 1 + blk["t0"]:1 + blk["t0"] + n], z[:, 0:n], reads=[zs], writes=[sl("zT")])

            proj_fm(Wf, S_w, D, hT, hT_slot, blocks, groups, epi)

        def gates_phase(l, blocks):
            Wf, S_w = wslice(l, "w_in")
            base = KVC + AQ + CQK + 3 * BW
            nt = 3 * D // P

            def groups(blk):
                return [(base + c * P, [c + i for i in range(min(4, nt - c))]) for c in range(0, nt, 4)]

            def epi(c, blk=None, tag=None, pt=None, pss=None):
                if c == "init":
                    return dict(o=Ring(T, blk, nc, "gso", [P, 512], BF16, 3))
                n = blk["n"]
                o, os_ = c["o"].next()
                T.op("act", "activation", dict(out=o[:, 0:n], in_=pt[:, 0:n], func=AF.Sigmoid), reads=[pss], writes=[os_])
                T.dma("act", sgT[tag * P:(tag + 1) * P, blk["col"]:blk["col"] + n], o[:, 0:n], reads=[os_],
                      writes=[sl(("sgT", blk["col"]))])

            proj_fm(Wf, S_w, D, hT, hT_slot, blocks, groups, epi)

        def conv_phase(l, blocks, which):
            if which == "b":
                nct, wct, src, srcc, mul, mulname, dst, drow0 = BW // P, wcb, zT, zTc, bbT, "bbT", yT, BR
            else:
                nct, wct, src, srcc, mul, mulname, dst, drow0 = DFF // P, wcf, gT, gTc, uT, "uT", aT, 0
            with ExitStack() as st:
                win = Ring(T, st, nc, "cwin", [P, 516], F32, 3)
                mu = Ring(T, st, nc, "cmu", [P, 512], BF16, 3)
                a1 = Ring(T, st, nc, "ca1", [P, 512], F32, 2)
                a2 = Ring(T, st, nc, "ca2", [P, 512], F32, 2)
                a3 = Ring(T, st, nc, "ca3", [P, 512], F32, 2)
                a4 = Ring(T, st, nc, "ca4", [P, 512], F32, 2)
                oo = Ring(T, st, nc, "coo", [P, 512], BF16, 3)
                for blk in blocks:
                    n = blk["n"]
                    isctx = blk["kind"] == "ctx"
                    for ci in range(nct):
                        rows = slice(ci * P, (ci + 1) * P)
                        wt, ws = win.next()
                        if isctx:
                            T.dma("sp", wt[:, 0:n + 2], srcc[rows, 0:n + 2], reads=[sl("zTc" if which == "b" else "gTc")], writes=[ws])
                        else:
                            T.dma("sp", wt[:, 0:n + 2], src[rows, blk["t0"]:blk["t0"] + n + 2],
                                  reads=[sl("zT" if which == "b" else "gT")], writes=[ws])
                        mt, ms = mu.next()
                        T.dma("sp", mt[:, 0:n], mul[rows, blk["col"]:blk["col"] + n], reads=[sl((mulname, blk["col"]))], writes=[ms])
                        b1, b1s = a1.next()
                        T.op("dve", "tensor_scalar", dict(out=b1[:, 0:n], in0=wt[:, 0:n], scalar1=wct[:, l, 0, ci:ci + 1],
                                                                                  scalar2=None, op0=ALU.mult), reads=[ws, S_c], writes=[b1s])
                        b2, b2s = a2.next()
                        T.op("dve", "scalar_tensor_tensor", dict(
                            out=b2[:, 0:n], in0=wt[:, 1:n + 1], scalar=wct[:, l, 1, ci:ci + 1], in1=b1[:, 0:n], op0=ALU.mult, op1=ALU.add),
                            reads=[ws, b1s, S_c], writes=[b2s])
                        b3, b3s = a3.next()
                        T.op("dve", "scalar_tensor_tensor", dict(
                            out=b3[:, 0:n], in0=wt[:, 2:n + 2], scalar=wct[:, l, 2, ci:ci + 1], in1=b2[:, 0:n], op0=ALU.mult, op1=ALU.add),
                            reads=[ws, b2s, S_c], writes=[b3s])
                        o, os_ = oo.next()
                        if which == "f":
                            b4, b4s = a4.next()
                            T.op("act", "activation", dict(out=b4[:, 0:n], in_=b3[:, 0:n], func=AF.Silu),
                                 reads=[b3s], writes=[b4s])
                            b3, b3s = b4, b4s
                        T.op("dve", "tensor_tensor", dict(out=o[:, 0:n], in0=b3[:, 0:n], in1=mt[:, 0:n], op=ALU.mult),
                             reads=[b3s, ms], writes=[os_])
                        T.dma("act", dst[drow0 + ci * P:drow0 + (ci + 1) * P, blk["col"]:blk["col"] + n], o[:, 0:n], reads=[os_],
                              writes=[sl(("yT" if which == "b" else "aT", blk["col"]))])
                T.barrier()

        def attn_phase(l, blocks):
            with ExitStack() as st:
                Kr = Ring(T, st, nc, "aK", [P, 2, NKEY], BF16, 2)
                Vr = Ring(T, st, nc, "aV", [P, NKEYT, 256], BF16, 1)
                Qr = Ring(T, st, nc, "aQ", [P, max(G, 2), 512], BF16, 2)
                Sp = Ring(T, st, nc, "aS", [P, 512], F32, 4, psum=True)
                Pt = Ring(T, st, nc, "aP", [P, 512], BF16, 6)
                Oa = Ring(T, st, nc, "aOa", [P, 512], F32, 1, psum=True)
                Ob = Ring(T, st, nc, "aOb", [P, 512], F32, 1, psum=True)
                Sm = Ring(T, st, nc, "aSm", [P, 512], F32, 1, psum=True)
                Lp = Ring(T, st, nc, "aLp", [P, 512], F32, 1, psum=True)
                rs = Ring(T, st, nc, "ars", [P, 512], F32, 2)
                f1 = Ring(T, st, nc, "af1", [P, 2, 512], F32, 2)
                f2 = Ring(T, st, nc, "af2", [P, 2, 512], F32, 2)
                tq = Ring(T, st, nc, "atq", [P, 512], F32, 2)
                sqr = Ring(T, st, nc, "asq", [P, 2, 512], BF16, 2)
                lnr = Ring(T, st, nc, "aln", [P, 512], F32, 2)
                yo = Ring(T, st, nc, "ayo", [P, 512], BF16, 3)
                KA = KT_all.rearrange("(i r j d) t -> d i r j t", r=4, j=TPC, d=P)
                VA = V_all.rearrange("(i r p) c -> p r i c", r=4, p=P)
                VCx = V_ctx.rearrange("(k p) c -> p k c", p=P)
                nkc = CTX // P
                NI = TL // P
                S_ka = [sl(("KT_all", ci)) for ci in range(NKT // TPC)]
                S_va = [sl(("V_all", ci)) for ci in range(TL // P)]

                def load_K(Kt, Ks, j, tile):
                    T.dma("sp", Kt[:, j, 0:CTX], KT_ctx[tile * P:(tile + 1) * P, :], reads=[sl("KT_ctx")], writes=[Ks])
                    for r in range(4):
                        T.dma("sp", Kt[:, j, CTX + r * TL:CTX + (r + 1) * TL], KA[:, tile // TPC, r, tile % TPC, :],
                              reads=[S_ka[tile // TPC]], writes=[Ks])

                def load_V(Vt, Vs, c0, w):
                    T.dma("sp", Vt[:, 0:nkc, 0:w], VCx[:, :, c0:c0 + w], reads=[sl("V_ctx")], writes=[Vs])
                    for r in range(4):
                        T.dma("sp", Vt[:, nkc + r * NI:nkc + (r + 1) * NI, 0:w], VA[:, r, :, c0:c0 + w], reads=S_va, writes=[Vs])

                def softmax_pv(Kt, Ks, j, Vt, Vs, Qt, Qs, qi, n, nkt, dvs):
                    accs = [Oa.next(), Ob.next()][:len(dvs)]
                    sm, sms = Sm.next()
                    LOOK = 3
                    pend = {}

                    def emit_S(kt):
                        sp_, sps = Sp.next()
                        T.op("pe", "matmul", dict(out=sp_[:, 0:n], lhsT=Kt[:, j, kt * P:(kt + 1) * P], rhs=Qt[:, qi, 0:n],
                                                  start=True, stop=True), reads=[Ks, Qs], writes=[sps])
                        pend[kt] = (sp_, sps)

                    for kt in range(min(LOOK, nkt)):
                        emit_S(kt)
                    for kt in range(nkt):
                        if kt + LOOK < nkt:
                            emit_S(kt + LOOK)
                        sp_, sps = pend.pop(kt)
                        pt, pts = Pt.next()
                        T.op("act", "activation", dict(out=pt[:, 0:n], in_=sp_[:, 0:n], func=AF.Exp, scale=SCALE),
                             reads=[sps], writes=[pts])
                        fl = (kt == 0 or kt == nkt - 1)
                        for (acc, accs_), dv in zip(accs, dvs):
                            T.op("pe", "matmul", dict(out=acc[:, 0:n], lhsT=Vt[:, kt, dv:dv + P], rhs=pt[:, 0:n],
                                                      start=(kt == 0), stop=(kt == nkt - 1)),
                                 reads=[pts, Vs], writes=[accs_] if fl else [])
                        T.op("pe", "matmul", dict(out=sm[:, 0:n], lhsT=cbf[:, 2, :], rhs=pt[:, 0:n],
                                                  start=(kt == 0), stop=(kt == nkt - 1)),
                             reads=[pts, S_c], writes=[sms] if fl else [])
                    return accs, (sm, sms)

                for hk in range(AKVH):
                    Kt, Ks = Kr.next()
                    load_K(Kt, Ks, 0, hk)
                    Vt, Vs = Vr.next()
                    load_V(Vt, Vs, hk * P, P)
                    for blk in blocks:
                        n = blk["n"]
                        nkt = nkc if blk["kind"] == "ctx" else NKEYT
                        Qt, Qs = Qr.next()
                        T.dma("sp", Qt[:, 0:G, 0:n],
                              QT.rearrange("(j d) t -> d j t", d=P)[:, hk * G:(hk + 1) * G, blk["col"]:blk["col"] + n],
                              reads=[sl(("QT", blk["col"]))], writes=[Qs])
                        for g in range(G):
                            accs, (sm, sms) = softmax_pv(Kt, Ks, 0, Vt, Vs, Qt, Qs, g, n, nkt, [0])
                            r_, rs_ = rs.next()
                            T.op("dve", "reciprocal", dict(out=r_[:, 0:n], in_=sm[:, 0:n]), reads=[sms], writes=[rs_])
                            o, os_ = yo.next()
                            acc, accs_ = accs[0]
                            T.op("dve", "tensor_tensor", dict(out=o[:, 0:n], in0=acc[:, 0:n], in1=r_[:, 0:n], op=ALU.mult),
                                 reads=[accs_, rs_], writes=[os_])
                            qi = hk * G + g
                            T.dma("act", yT[qi * P:(qi + 1) * P, blk["col"]:blk["col"] + n], o[:, 0:n], reads=[os_],
                                  writes=[sl(("yT", blk["col"]))])
                for h in range(CH):
                    Kt, Ks = Kr.next()
                    load_K(Kt, Ks, 0, AKVH + 2 * h)
                    load_K(Kt, Ks, 1, AKVH + 2 * h + 1)
                    Vt, Vs = Vr.next()
                    load_V(Vt, Vs, AKV + h * 256, 256)
                    for blk in blocks:
                        n = blk["n"]
                        nkt = nkc if blk["kind"] == "ctx" else NKEYT
                        Qt, Qs = Qr.next()
                        T.dma("sp", Qt[:, 0:2, 0:n],
                              QT.rearrange("(j d) t -> d j t", d=P)[:, AQH + 2 * h:AQH + 2 * h + 2, blk["col"]:blk["col"] + n],
                              reads=[sl(("QT", blk["col"]))], writes=[Qs])
                        accs, (sm, sms) = softmax_pv(Kt, Ks, 0, Vt, Vs, Qt, Qs, 0, n, nkt, [0, P])
                        r_, rs_ = rs.next()
                        T.op("dve", "reciprocal", dict(out=r_[:, 0:n], in_=sm[:, 0:n]), reads=[sms], writes=[rs_])
                        o1, o1s = f1.next()
                        for hf in range(2):
                            acc, accs_ = accs[hf]
                            T.op("dve", "tensor_tensor", dict(out=o1[:, hf, 0:n], in0=acc[:, 0:n], in1=r_[:, 0:n], op=ALU.mult),
                                 reads=[accs_, rs_], writes=[o1s])
                        accs, (sm, sms) = softmax_pv(Kt, Ks, 1, Vt, Vs, Qt, Qs, 1, n, nkt, [0, P])
                        r2, r2s = rs.next()
                        T.op("dve", "reciprocal", dict(out=r2[:, 0:n], in_=sm[:, 0:n]), reads=[sms], writes=[r2s])
                        o2, o2s = f2.next()
                        sq, sqs = sqr.next()
                        for hf in range(2):
                            acc, accs_ = accs[hf]
                            t_, ts_ = tq.next()
                            T.op("dve", "tensor_tensor", dict(out=t_[:, 0:n], in0=acc[:, 0:n], in1=r2[:, 0:n], op=ALU.mult),
                                 reads=[accs_, r2s], writes=[ts_])
                            T.op("dve", "scalar_tensor_tensor", dict(
                                out=o2[:, hf, 0:n], in0=t_[:, 0:n], scalar=nlam[:, l:l + 1], in1=o1[:, hf, 0:n], op0=ALU.mult, op1=ALU.add),
                                reads=[ts_, o1s, S_c], writes=[o2s])
                            T.op("act", "activation", dict(out=sq[:, hf, 0:n], in_=o2[:, hf, 0:n], func=AF.Square),
                                 reads=[o2s], writes=[sqs])
                        lp, lps = Lp.next()
                        T.mm_group(lps, [(dict(out=lp[:, 0:n], lhsT=cbf[:, 2, :], rhs=sq[:, hf, 0:n],
                                                                                 start=(hf == 0), stop=(hf == 1)), [sqs, S_c]) for hf in range(2)])
                        ln, lns = lnr.next()
                        T.op("act", "activation", dict(out=ln[:, 0:n], in_=lp[:, 0:n], func=AF.Ln, scale=1.0 / 256, bias=epsb[:, 0:1]),
                             reads=[lps, S_c], writes=[lns])
                        rr, rrs = rs.next()
                        T.op("act", "activation", dict(out=rr[:, 0:n], in_=ln[:, 0:n], func=AF.Exp, scale=-0.5),
                             reads=[lns], writes=[rrs])
                        for hf in range(2):
                            o, os_ = yo.next()
                            T.op("dve", "scalar_tensor_tensor", dict(
                                out=o[:, 0:n], in0=o2[:, hf, 0:n], scalar=gsc[:, 2 * l + hf:2 * l + hf + 1], in1=rr[:, 0:n],
                                op0=ALU.mult, op1=ALU.mult), reads=[o2s, rrs, S_c], writes=[os_])
                            r0 = 2 * BR + h * 256 + hf * P
                            T.dma("act", yT[r0:r0 + P, blk["col"]:blk["col"] + n], o[:, 0:n], reads=[os_],
                                  writes=[sl(("yT", blk["col"]))])
                T.barrier()

        def merge_phase(l, blocks):
            Wf, S_w = wslice(l, "w_br")
            kb = BR // P
            with ExitStack() as st:
                wr = Ring(T, st, nc, "mw", [P, 3 * kb, 256], BF16, 2)
                at = sb(st, "ma", [P, 3 * kb, 512], BF16)
                as_ = Slot()
                pr_ = Ring(T, st, nc, "mps", [P, 512], F32, 6, psum=True)
                sg = Ring(T, st, nc, "msg", [P, 3, 512], BF16, 3)
                t0r = Ring(T, st, nc, "mt0", [P, 512], F32, 2)
                t1r = Ring(T, st, nc, "mt1", [P, 512], F32, 2)
                t2r = Ring(T, st, nc, "mt2", [P, 512], F32, 2)
                t3r = Ring(T, st, nc, "mt3", [P, 512], F32, 2)
                mo = Ring(T, st, nc, "mmo", [P, 512], BF16, 3)
                for blk in blocks:
                    n = blk["n"]
                    T.dma("sp", at[:, :, 0:n], yT.rearrange("(k p) t -> p k t", p=P)[:, :, blk["col"]:blk["col"] + n],
                          reads=[sl(("yT", blk["col"]))], writes=[as_])
                    for dt0 in range(0, D // P, 2):
                        wt, ws = wr.next()
                        T.dma("sp", wt[:, :, :], Wf.rearrange("(k p) c -> p k c", p=P)[:, :, dt0 * P:(dt0 + 2) * P], reads=list(S_w), writes=[ws])
                        for di in range(2):
                            dt_ = dt0 + di
                            pss_ = []
                            for nb in range(3):
                                pt, pss = pr_.next()
                                T.mm_group(pss, [(dict(out=pt[:, 0:n], lhsT=wt[:, nb * kb + k, di * P:(di + 1) * P], rhs=at[:, nb * kb + k, 0:n],
                                    start=(k == 0), stop=(k == kb - 1)), [ws, as_]) for k in range(kb)])
                                pss_.append((pt, pss))
                            sgt, sgs = sg.next()
                            T.dma("sp", sgt[:, :, 0:n], sgT.rearrange("(b j p) t -> p b j t", b=3, p=P)[:, :, dt_, blk["col"]:blk["col"] + n],
                                  reads=[sl(("sgT", blk["col"]))], writes=[sgs])
                            a0, a0s = t0r.next()
                            T.op("dve", "tensor_tensor", dict(out=a0[:, 0:n], in0=pss_[0][0][:, 0:n], in1=sgt[:, 0, 0:n], op=ALU.mult),
                                 reads=[pss_[0][1], sgs], writes=[a0s])
                            a1, a1s = t1r.next()
                            T.op("dve", "tensor_tensor", dict(out=a1[:, 0:n], in0=pss_[1][0][:, 0:n], in1=sgt[:, 1, 0:n], op=ALU.mult),
                                 reads=[pss_[1][1], sgs], writes=[a1s])
                            a2, a2s = t2r.next()
                            T.op("dve", "tensor_tensor", dict(out=a2[:, 0:n], in0=pss_[2][0][:, 0:n], in1=sgt[:, 2, 0:n], op=ALU.mult),
                                 reads=[pss_[2][1], sgs], writes=[a2s])
                            a3, a3s = t3r.next()
                            T.op("dve", "tensor_tensor", dict(out=a3[:, 0:n], in0=a0[:, 0:n], in1=a1[:, 0:n], op=ALU.add),
                                 reads=[a0s, a1s], writes=[a3s])
                            o, os_ = mo.next()
                            T.op("dve", "tensor_tensor", dict(out=o[:, 0:n], in0=a3[:, 0:n], in1=a2[:, 0:n], op=ALU.add),
                                 reads=[a3s, a2s], writes=[os_])
                            T.dma("act", mT[dt_ * P:(dt_ + 1) * P, blk["col"]:blk["col"] + n], o[:, 0:n], reads=[os_],
                                  writes=[sl(("mT", blk["col"]))])
                T.barrier()

        def resid_phase(l, nm, Krows, actT, actname, blocks, gsec, xold_of, xnew_of):
            Wf, S_w = wslice(l, nm)
            cgs = [(c, c, 512) for c in range(0, D, 512)]

            def epi(c, blk=None, sub=None, r=None, cg=None, pt=None, pss=None):
                if c == "init":
                    st = blk
                    o = dict(gr=Ring(T, st, nc, "rg", [P, 512], F32, 3), gcur=[None, None],
                             xo_=Ring(T, st, nc, "rxo", [P, 512], F32, 3), t=Ring(T, st, nc, "rt", [P, 512], F32, 2),
                             o=Ring(T, st, nc, "ro", [P, 512], F32, 3))
                    return o
                wc0, oc0, w = cg
                if sub == 0:
                    v = 1 if blk["kind"] == "ctx" else 0
                    gt, gs_ = c["gr"].next()
                    T.dma("sp", gt[:, :], modbc[l * 12 + v * 6 + gsec][:, oc0:oc0 + w], reads=[sl(("modbc", l * 12 + v * 6 + gsec))], writes=[gs_])
                    c["gcur"] = [gt, gs_]
                gt, gs_ = c["gcur"]
                xsrc, xsl = xold_of(blk)
                xdst, xdl = xnew_of(blk)
                xt, xs = c["xo_"].next()
                T.dma("sp", xt[0:r, :], xsrc[sub * P:sub * P + r, oc0:oc0 + w], reads=xsl, writes=[xs])
                t_, ts_ = c["t"].next()
                T.op("dve", "tensor_tensor", dict(out=t_[0:r, :], in0=pt[0:r, :], in1=gt[0:r, :], op=ALU.mult),
                     reads=[pss, gs_], writes=[ts_])
                o, os_ = c["o"].next()
                T.op("dve", "tensor_tensor", dict(out=o[0:r, :], in0=t_[0:r, :], in1=xt[0:r, :], op=ALU.add),
                     reads=[ts_, xs], writes=[os_])
                T.dma("act", xdst[sub * P:sub * P + r, oc0:oc0 + w], o[0:r, :], reads=[os_], writes=xdl)

            proj_tm(Wf, S_w, Krows, actT, lambda blk: sl((actname, blk["col"])), blocks, cgs, epi)

        def halo_phase(xsrc, xsl):
            with ExitStack() as st:
                ha = sb(st, "ha", [8, D])
                ho = sb(st, "ho", [2, D])
                hp = Ring(T, st, nc, "hps", [P, 512], F32, 2, psum=True)
                s_ha, s_ho = Slot(), Slot()
                T.dma("act", hb_in[0:1, :], xsrc[CTX:CTX + 1, :], reads=xsl, writes=[sl("hb_in")])
                T.dma("act", hb_in[1:2, :], xsrc[CTX + TL - 1:CTX + TL, :], reads=xsl, writes=[sl("hb_in")])
                T.cc(G4, hb_in, hb_all, reads=[sl("hb_in")], writes=[sl("hb_all")])
                T.dma("sp", ha[:, :], hb_all, reads=[sl("hb_all")], writes=[s_ha])
                for ch in range(D // 512):
                    pt, pss = hp.next()
                    T.mm_group(pss, [(dict(out=pt[0:2, :], lhsT=selH[:, :], rhs=ha[:, ch * 512:(ch + 1) * 512],
                                                                       start=True, stop=True), [s_ha, S_c])])
                    T.op("act", "activation", dict(out=ho[0:2, ch * 512:(ch + 1) * 512], in_=pt[0:2, :], func=AF.Copy),
                         reads=[pss], writes=[s_ho])
                T.dma("act", xhs, ho[:, :], reads=[s_ho], writes=[sl("xhs")])
                T.barrier()

        def up_phase(l, blocks):
            Wf, S_w = wslice(l, "w_up")
            nct = DFF // P

            def groups(blk):
                out = []
                for c in range(0, nct, 4):
                    m = min(4, nct - c)
                    if blk["kind"] != "halo":
                        out.append((c * P, [("u", c + i) for i in range(m)]))
                    out.append((DFF + c * P, [("g", c + i) for i in range(m)]))
                return out

            def epi(c, blk=None, tag=None, pt=None, pss=None):
                if c == "init":
                    st = blk
                    return dict(uo=Ring(T, st, nc, "uuo", [P, 512], BF16, 3), go=Ring(T, st, nc, "ugo", [P, 512], F32, 3))
                n = blk["n"]
                kind, ci = tag
                rows = slice(ci * P, (ci + 1) * P)
                if kind == "u":
                    o, os_ = c["uo"].next()
                    T.op("act", "activation", dict(out=o[:, 0:n], in_=pt[:, 0:n], func=AF.Copy), reads=[pss], writes=[os_])
                    T.dma("act", uT[rows, blk["col"]:blk["col"] + n], o[:, 0:n], reads=[os_], writes=[sl(("uT", blk["col"]))])
                else:
                    o, os_ = c["go"].next()
                    if blk["kind"] == "halo":
                        T.op("dve", "tensor_tensor", dict(out=o[:, 0:2], in0=pt[:, 0:2], in1=hmask[:, 0:2], op=ALU.mult),
                             reads=[pss, S_c], writes=[os_])
                        T.dma("act", gT[rows, 0:1], o[:, 0:1], reads=[os_], writes=[sl("gT")])
                        T.dma("act", gT[rows, TL + 1:TL + 2], o[:, 1:2], reads=[os_], writes=[sl("gT")])
                    else:
                        T.op("act", "activation", dict(out=o[:, 0:n], in_=pt[:, 0:n], func=AF.Copy), reads=[pss], writes=[os_])
                        if blk["kind"] == "ctx":
                            T.dma("act", gTc[rows, 1:1 + n], o[:, 0:n], reads=[os_], writes=[sl("gTc")])
                        else:
                            T.dma("act", gT[rows, 1 + blk["t0"]:1 + blk["t0"] + n], o[:, 0:n], reads=[os_], writes=[sl("gT")])

            proj_fm(Wf, S_w, D, hT, hT_slot, blocks, groups, epi)

        epsb = sb(es, "epsb", [P, 1])
        T.op("dve", "memset", dict(ap=epsb[:, :], constant=EPS), writes=[S_c])

        def rows_of(lat_ap, ctx_ap, halo_ap, lat_sl, ctx_sl, halo_sl):
            def f(blk):
                if blk["kind"] == "lat":
                    return lat_ap[blk["t0"]:blk["t0"] + blk["n"], :], lat_sl
                if blk["kind"] == "ctx":
                    return ctx_ap, ctx_sl
                return halo_ap, halo_sl
            return f

        import os as _os
        _lim = int(_os.environ.get("MK_STOP_AFTER", "100000"))
        _cnt = [0]

        def _wrap(f):
            def g(*a, **k):
                _cnt[0] += 1
                if _cnt[0] <= _lim:
                    return f(*a, **k)
            return g
        mod_phase, halo_phase, norm_phase, qk_phase, v_phase, restb_phase, gates_phase, conv_phase, attn_phase, \
            merge_phase, resid_phase, up_phase = [_wrap(f) for f in (
                mod_phase, halo_phase, norm_phase, qk_phase, v_phase, restb_phase, gates_phase, conv_phase, attn_phase,
                merge_phase, resid_phase, up_phase)]
        _tcc = _wrap(T.cc)
        for l in range(L):
            last = l == L - 1
            allb = [ctx_block] + lat_blocks + [halo_block]
            main = lat_blocks if last else [ctx_block] + lat_blocks
            if l == 0:
                for l2 in range(L):
                    mod_phase(l2)
                weights_phase(0)
            if l == 0:
                src1 = rows_of(x_in, ctx_in, xh_in, [], [], [])
                xold1 = lambda blk: ((x_in[blk["t0"]:blk["t0"] + blk["n"], :], []) if blk["kind"] == "lat" else (ctx_in, []))
            else:
                halo_phase(xo, [sl("xo")])
                src1 = rows_of(xo[CTX:CTX + TL, :], xo[0:CTX, :], xhs, [sl("xo")], [sl("xo")], [sl("xhs")])
                xold1 = lambda blk: ((xo[CTX + blk["t0"]:CTX + blk["t0"] + blk["n"], :], [sl("xo")]) if blk["kind"] == "lat"
                                     else (xo[0:CTX, :], [sl("xo")]))
            norm_phase(l, "mix", src1, allb)
            qk_phase(l, False, [ctx_block] + lat_blocks)
            v_phase(l, [ctx_block] + lat_blocks)
            for ci in range(NKT // TPC):
                rr_ = TPC * P
                _tcc(G4, KT_loc[ci * rr_:(ci + 1) * rr_, :], KT_all[ci * 4 * rr_:(ci + 1) * 4 * rr_, :],
                     reads=[sl("KT_loc")], writes=[sl(("KT_all", ci))])
            for ci in range(TL // P):
                _tcc(G4, V_loc[ci * P:(ci + 1) * P, :], V_all[ci * 4 * P:(ci + 1) * 4 * P, :],
                     reads=[sl("V_loc")], writes=[sl(("V_all", ci))])
            if l + 1 < L:
                weights_phase(l + 1)
            qk_phase(l, True, main)
            restb_phase(l, main + [halo_block])
            gates_phase(l, main)
            conv_phase(l, main, "b")
            attn_phase(l, main)
            merge_phase(l, main)
            xm_of = lambda blk: ((xm[CTX + blk["t0"]:CTX + blk["t0"] + blk["n"], :], [sl("xm")]) if blk["kind"] == "lat"
                                 else (xm[0:CTX, :], [sl("xm")]))
            resid_phase(l, "w_out", D, mT, "mT", main, 2, xold1, xm_of)
            halo_phase(xm, [sl("xm")])
            src2 = rows_of(xm[CTX:CTX + TL, :], xm[0:CTX, :], xhs, [sl("xm")], [sl("xm")], [sl("xhs")])
            norm_phase(l, "ffn", src2, main + [halo_block])
            up_phase(l, main + [halo_block])
            conv_phase(l, main, "f")
            if last:
                xn_of = lambda blk: (y_out[blk["t0"]:blk["t0"] + blk["n"], :], [sl("y")])
            else:
                xn_of = lambda blk: ((xo[CTX + blk["t0"]:CTX + blk["t0"] + blk["n"], :], [sl("xo")]) if blk["kind"] == "lat"
                                     else (xo[0:CTX, :], [sl("xo")]))
            resid_phase(l, "w_down", DFF, aT, "aT", main, 5, xm_of, xn_of)
        T.barrier(full=True)
        T.replay()
    return nc


def make_in_maps(cfg, inp):
    D, TL, CTX, S = cfg.D, cfg.TL, cfg.CTX, cfg.S
    L = DEPTH
    f32 = np.float32
    x = np.asarray(inp["x"], f32)
    ctx = np.asarray(inp["ctx"], f32)
    c = np.asarray(inp["c"], f32)
    c_ctx = np.asarray(inp["c_ctx"], f32)
    cvec = np.stack([c[0], c[1], c_ctx], -1)
    D8, F8, B8 = D // 8, cfg.DFF // 8, 3 * cfg.BR // 8
    kc8 = D8 // P
    w_br = np.asarray(inp["w_branch"], f32).reshape(L, 3 * cfg.BR, D)
    ident = np.eye(P, dtype=f32)
    RT = np.zeros((P, P), f32)
    for d in range(P):
        if (d % 64) < 32:
            RT[d + 32, d] = -1.0
        else:
            RT[d - 32, d] = 1.0
    consts = np.stack([ident, RT, np.ones((P, P), f32)])
    quarter = P // 4
    freqs = (ROPE_THETA ** (-np.arange(quarter, dtype=np.float32) / quarter)).astype(f32)

    def rep(v):
        return np.ascontiguousarray(np.broadcast_to(np.asarray(v, f32)[:, None, :], (L, P, v.shape[-1])))

    gq = np.zeros((P, L * 3), f32)
    gk = np.zeros((P, L * 3), f32)
    for l in range(L):
        gq[:, l * 3 + 0] = inp["g_qa"][l]
        gq[:, l * 3 + 1] = inp["g_qc"][l, 0]
        gq[:, l * 3 + 2] = inp["g_qc"][l, 1]
        gk[:, l * 3 + 0] = inp["g_ka"][l]
        gk[:, l * 3 + 1] = inp["g_kc"][l, 0]
        gk[:, l * 3 + 2] = inp["g_kc"][l, 1]
    wcb = np.ascontiguousarray(np.asarray(inp["w_conv_b"], f32).reshape(L, 3, cfg.BW // P, P).transpose(3, 0, 1, 2))
    wcf = np.ascontiguousarray(np.asarray(inp["w_conv_ffn"], f32).reshape(L, 3, cfg.DFF // P, P).transpose(3, 0, 1, 2))
    lamq = np.ascontiguousarray(np.asarray(inp["lam_qk"], f32).reshape(L * 4, P).T)
    gsub = np.ascontiguousarray(np.asarray(inp["g_subln"], f32).reshape(L * 2, P).T)
    g_mix = rep(inp["g_norm_mix"])
    g_ffn = rep(inp["g_norm_ffn"])
    b_mod = np.ascontiguousarray(np.asarray(inp["b_mod"], f32)[:, None, :])
    maps = []
    for core in range(NCORE):
        b, q = core // 4, core % 4
        s, e = q * TL, (q + 1) * TL
        xh = np.zeros((2, D), f32)
        hm = np.zeros((P, 2), f32)
        if s > 0:
            xh[0] = x[b, s - 1]
            hm[:, 0] = 1.0
        if e < S:
            xh[1] = x[b, e]
            hm[:, 1] = 1.0
        selB = np.zeros((25, 2, P), f32)
        for r in range(8):
            selB[r * 3 + b, 0, :] = 1.0
            selB[r * 3 + 2, 1, :] = 1.0
        selB[24, :, :] = 1.0
        selH = np.zeros((8, 2), f32)
        if q > 0:
            selH[2 * (q - 1) + 1, 0] = 1.0
        if q < 3:
            selH[2 * (q + 1), 1] = 1.0
        pos = np.arange(s, e)
        row = (pos // GW).astype(f32)
        colp = (pos % GW).astype(f32)
        ang_r = row[:, None] * freqs
        ang_c = colp[:, None] * freqs
        ang = np.concatenate([ang_r, ang_r, ang_c, ang_c], -1).astype(f32)
        cT = np.ascontiguousarray(cvec[core * D8:(core + 1) * D8].reshape(kc8, P, 3).transpose(1, 0, 2))
        m = {
            "x": np.ascontiguousarray(x[b, s:e]), "xh": xh, "ctx": np.ascontiguousarray(ctx[b]), "cT": cT,
            "w_mod": np.ascontiguousarray(inp["w_mod"][:, core * D8:(core + 1) * D8, :]), "b_mod": b_mod,
            "selB": selB, "selH": selH, "hmask": hm, "g_mix": g_mix, "g_ffn": g_ffn, "gq": gq, "gk": gk,
            "wcb": wcb, "wcf": wcf, "lamq": lamq, "gsub": gsub,
            "cosT": np.ascontiguousarray(np.cos(ang).T.astype(f32)), "sinT": np.ascontiguousarray(np.sin(ang).T.astype(f32)),
            "consts": consts,
            "w_in": np.ascontiguousarray(inp["w_in"][:, shard_rows(D8, cfg.INC, b, q), :]),
            "w_br": np.ascontiguousarray(w_br[:, shard_rows(B8, D, b, q), :]),
            "w_out": np.ascontiguousarray(inp["w_out"][:, shard_rows(D8, D, b, q), :]),
            "w_up": np.ascontiguousarray(inp["w_up"][:, shard_rows(D8, 2 * cfg.DFF, b, q), :]),
            "w_down": np.ascontiguousarray(inp["w_down"][:, shard_rows(F8, D, b, q), :]),
        }
        maps.append(m)
    return maps


def run(cfg, inp):
    nc = build(cfg)
    maps = make_in_maps(cfg, inp)
    res = run_bass_kernel_spmd(nc, maps, core_ids=list(range(NCORE)))
    out = np.zeros((2, cfg.S, cfg.D), np.float32)
    for core in range(NCORE):
        b, q = core // 4, core % 4
        out[b, q * cfg.TL:(q + 1) * cfg.TL] = res.results[core]["y"]
    return out


def kernel(**inputs):
    inp = {k: np.asarray(v) for k, v in inputs.items()}
    return run(Cfg(), inp)
```
